# Optimizing a Trainium2 kernel written in Bass

```python
import math
import jax, jax.numpy as jnp
from jax import lax
import numpy as np

D_MODEL = 2048
BATCH = 4
SEQ = 2048
DEPTH = 1

N_HEADS_A = 8
HEAD_DIM_A = 128
WIDTH_A = N_HEADS_A * HEAD_DIM_A
N_IDX_HEADS = 16
IDX_DIM = 64
TOPK_MAX = 256
Q_BLOCK = 128
N_GROUPS_B = 8
GROUP_DIM_B = 128
WIDTH_B = N_GROUPS_B * GROUP_DIM_B
CHUNK = 128
D_FF = ((8 * D_MODEL + 3 * 256 - 1) // (3 * 256)) * 256
N_BUCKETS = 32
MAX_DISTANCE = 128
EPS = 1e-6
IN_SPLITS = (WIDTH_A, WIDTH_A, WIDTH_A, N_IDX_HEADS * IDX_DIM, IDX_DIM, N_IDX_HEADS,
             WIDTH_B, WIDTH_B, D_MODEL, D_MODEL)
D_IN = sum(IN_SPLITS)

kernel_name = "hybrid_dsa_gmlp_gated_block"


def _rmsnorm(x, g):
    xf = x.astype(jnp.float32)
    y = xf * lax.rsqrt(jnp.mean(xf * xf, axis=-1, keepdims=True) + EPS)
    return (y * g.astype(jnp.float32)).astype(x.dtype)


def _t5_bucket(dist):
    max_exact = N_BUCKETS // 2
    d = jnp.maximum(dist.astype(jnp.float32), 1.0)
    large = max_exact + (jnp.log(d / max_exact) / math.log(MAX_DISTANCE / max_exact)
                         * (N_BUCKETS - max_exact)).astype(jnp.int32)
    large = jnp.minimum(large, N_BUCKETS - 1)
    return jnp.where(dist < max_exact, dist, large)


def _split_cols(proj):
    offs = np.cumsum(IN_SPLITS)[:-1].tolist()
    return jnp.split(proj, offs, axis=-1)


def _sparse_attention(q, k, v, q_idx, k_idx, w_idx, rel_bias):
    B, S = q.shape[0], q.shape[1]
    n_sel = min(TOPK_MAX, S // 4)
    n_blocks = S // Q_BLOCK
    scale = HEAD_DIM_A ** -0.5
    idx_scale = (IDX_DIM ** -0.5) * (N_IDX_HEADS ** -0.5)
    key_pos = jnp.arange(S, dtype=jnp.int32)
    b_ix = jnp.arange(B)[:, None, None]
    k_idx_f = k_idx.astype(jnp.float32)

    def block(i):
        t0 = i * Q_BLOCK
        qb = lax.dynamic_slice_in_dim(q, t0, Q_BLOCK, axis=1)
        qib = lax.dynamic_slice_in_dim(q_idx, t0, Q_BLOCK, axis=1).astype(jnp.float32)
        wib = lax.dynamic_slice_in_dim(w_idx, t0, Q_BLOCK, axis=1).astype(jnp.float32)
        q_pos = t0 + jnp.arange(Q_BLOCK, dtype=jnp.int32)
        head_scores = jax.nn.relu(jnp.einsum('bqhd,bsd->bqhs', qib, k_idx_f))
        score = jnp.einsum('bqhs,bqh->bqs', head_scores, wib) * idx_scale
        causal = key_pos[None, :] <= q_pos[:, None]
        score = jnp.where(causal[None], score, -jnp.inf)
        _, sel = lax.top_k(score, n_sel)
        k_sel = k[b_ix, sel]
        v_sel = v[b_ix, sel]
        dist = q_pos[None, :, None] - sel
        valid = dist >= 0
        bias = rel_bias[_t5_bucket(jnp.maximum(dist, 0))]
        logits = (jnp.einsum('bqhd,bqkhd->bhqk', qb, k_sel).astype(jnp.float32) * scale
                  + bias.astype(jnp.float32).transpose(0, 3, 1, 2))
        logits = jnp.where(valid[:, None], logits, -jnp.inf)
        p = jax.nn.softmax(logits, axis=-1).astype(v.dtype)
        return jnp.einsum('bhqk,bqkhd->bqhd', p, v_sel)

    out = lax.map(block, jnp.arange(n_blocks, dtype=jnp.int32))
    return out.transpose(1, 0, 2, 3, 4).reshape(B, S, WIDTH_A)


def _chunked_sgu(u, v, w_spatial, b_spatial, norm_g):
    B, S = u.shape[0], u.shape[1]
    v = _rmsnorm(v, norm_g)
    vc = v.reshape(B, S // CHUNK, CHUNK, N_GROUPS_B, GROUP_DIM_B)
    mask = jnp.tril(jnp.ones((CHUNK, CHUNK), dtype=bool))
    w = jnp.where(mask[None], w_spatial, 0.0)
    z = jnp.einsum('gts,bcsgd->bctgd', w, vc) + b_spatial.T[:, :, None]
    return u * z.reshape(B, S, WIDTH_B)


def setup_inputs(seed: int = 0) -> dict:
    key = jax.random.key(seed)
    ks = jax.random.split(key, 17)

    def nrm(k, shape, scale):
        return jax.random.normal(k, shape, jnp.float32) * scale

    return {
        "x": nrm(ks[0], (BATCH, SEQ, D_MODEL), 1.0),
        "norm1_g": 1.0 + nrm(ks[1], (DEPTH, D_MODEL), 0.02),
        "w_in": nrm(ks[2], (DEPTH, D_MODEL, D_IN), D_MODEL ** -0.5),
        "q_norm_g": 1.0 + nrm(ks[3], (DEPTH, HEAD_DIM_A), 0.02),
        "k_norm_g": 1.0 + nrm(ks[4], (DEPTH, HEAD_DIM_A), 0.02),
        "idx_k_norm_g": 1.0 + nrm(ks[5], (DEPTH, IDX_DIM), 0.02),
        "sgu_norm_g": 1.0 + nrm(ks[6], (DEPTH, WIDTH_B), 0.02),
        "w_spatial": nrm(ks[7], (DEPTH, N_GROUPS_B, CHUNK, CHUNK), 0.5 * CHUNK ** -0.5),
        "b_spatial": 1.0 + nrm(ks[8], (DEPTH, N_GROUPS_B, CHUNK), 0.1),
        "w_proj_a": nrm(ks[9], (DEPTH, WIDTH_A, D_MODEL), WIDTH_A ** -0.5),
        "w_proj_b": nrm(ks[10], (DEPTH, WIDTH_B, D_MODEL), WIDTH_B ** -0.5),
        "w_out": nrm(ks[11], (DEPTH, D_MODEL, D_MODEL), D_MODEL ** -0.5),
        "norm2_g": 1.0 + nrm(ks[12], (DEPTH, D_MODEL), 0.02),
        "w_ffn_gate": nrm(ks[13], (DEPTH, D_MODEL, D_FF), D_MODEL ** -0.5),
        "w_ffn_up": nrm(ks[14], (DEPTH, D_MODEL, D_FF), D_MODEL ** -0.5),
        "w_ffn_down": nrm(ks[15], (DEPTH, D_FF, D_MODEL), D_FF ** -0.5),
        "rel_bias": nrm(ks[16], (N_BUCKETS, N_HEADS_A), 0.5),
    }


def reference(x, norm1_g, w_in, q_norm_g, k_norm_g, idx_k_norm_g, sgu_norm_g,
              w_spatial, b_spatial, w_proj_a, w_proj_b, w_out, norm2_g,
              w_ffn_gate, w_ffn_up, w_ffn_down, rel_bias):
    B, S = x.shape[0], x.shape[1]
    for l in range(DEPTH):
        h = _rmsnorm(x, norm1_g[l])
        proj = h @ w_in[l]
        q, k, v, q_i, k_i, w_i, u_b, v_b, g_a, g_b = _split_cols(proj)
        q = _rmsnorm(q.reshape(B, S, N_HEADS_A, HEAD_DIM_A), q_norm_g[l])
        k = _rmsnorm(k.reshape(B, S, N_HEADS_A, HEAD_DIM_A), k_norm_g[l])
        v = v.reshape(B, S, N_HEADS_A, HEAD_DIM_A)
        q_i = q_i.reshape(B, S, N_IDX_HEADS, IDX_DIM)
        k_i = _rmsnorm(k_i, idx_k_norm_g[l])
        out_a = _sparse_attention(q, k, v, q_i, k_i, w_i, rel_bias)
        out_b = _chunked_sgu(jax.nn.gelu(u_b), jax.nn.gelu(v_b),
                             w_spatial[l], b_spatial[l], sgu_norm_g[l])
        merged = (jax.nn.sigmoid(g_a) * (out_a @ w_proj_a[l])
                  + jax.nn.sigmoid(g_b) * (out_b @ w_proj_b[l]))
        x = x + merged @ w_out[l]
        h = _rmsnorm(x, norm2_g[l])
        x = x + (jax.nn.silu(h @ w_ffn_gate[l]) * (h @ w_ffn_up[l])) @ w_ffn_down[l]
    return x
```

```python
import math
from contextlib import ExitStack
import numpy as np
import concourse.bass as bass
import concourse.mybir as mybir
from concourse.bass_utils import run_bass_kernel_spmd

F32 = mybir.dt.float32
BF16 = mybir.dt.bfloat16
AF = mybir.ActivationFunctionType
ALU = mybir.AluOpType
AX = mybir.AxisListType

D = 2048
S = 2048
NH = 8
DFF = 5632
EPS = 1e-6
NIT = 16
ENG = ['pe', 'act', 'dve', 'pool', 'sp']
KB = 1024


class Buf:
    __slots__ = ('w', 'r', 'name')

    def __init__(self, name=''):
        self.w = []
        self.r = {}
        self.name = name


class Prog:
    def __init__(self):
        self.q = {e: [] for e in ENG}
        self.cnt = {e: 0 for e in ENG}
        self.waited = {e: {} for e in ENG}
        self.dma_n = {}
        self.dma_last = {}
        self.NDS = 8

    def _deps(self, eng, reads, writes):
        deps = {}

        def add(h):
            if h is None:
                return
            k, v = h
            if k == 'pe' and eng == 'pe':
                return
            if deps.get(k, 0) < v:
                deps[k] = v
        for b in reads:
            for h in b.w:
                add(h)
        for b in writes:
            for h in b.w:
                add(h)
            for k, v in b.r.items():
                add((k, v))
        waits = []
        for k, v in deps.items():
            if self.waited[eng].get(k, 0) < v:
                self.waited[eng][k] = v
                waits.append((k, v))
        return waits

    def _post(self, h, reads, writes, append=False):
        for b in writes:
            if append:
                b.w.append(h)
            else:
                b.w = [h]
                b.r = {}
        for b in reads:
            if b.r.get(h[0], 0) < h[1]:
                b.r[h[0]] = h[1]

    def op(self, eng, fn, reads=(), writes=()):
        waits = self._deps(eng, reads, writes)
        self.cnt[eng] += 1
        h = (eng, self.cnt[eng])
        self.q[eng].append((waits, fn, (eng, 1)))
        self._post(h, reads, writes)
        return h

    def dma(self, qeng, fn, reads=(), writes=(), append=False):
        n = self.dma_n.get(qeng, 0)
        self.dma_n[qeng] = n + 1
        key = 'd_%s_%d' % (qeng, n % self.NDS)
        val = 16 * (n // self.NDS + 1)
        waits = self._deps(qeng, reads, () if append else writes)
        if n >= self.NDS and self.waited[qeng].get(key, 0) < val - 16:
            self.waited[qeng][key] = val - 16
            waits.append((key, val - 16))
        h = (key, val)
        self.dma_last[key] = val
        self.q[qeng].append((waits, fn, (key, 16)))
        self._post(h, reads, writes, append)
        return h

    def barrier(self):
        tgt = {e: self.cnt[e] for e in ENG if self.cnt[e] > 0}
        tgt.update(self.dma_last)
        for e in ENG:
            waits = []
            for k, v in tgt.items():
                if k == e:
                    continue
                if self.waited[e].get(k, 0) < v:
                    self.waited[e][k] = v
                    waits.append((k, v))
            if waits:
                self.q[e].append((waits, None, None))


def build(ncores=8, stage=99, dbg=False):
    nc = bass.Bass("TRN2", target_bir_lowering=False)
    P = Prog()

    def din(name, shape, dt=F32):
        return nc.dram_tensor(name, list(shape), dt, kind="ExternalInput").ap()

    x_all = din("x_all", [2048, 2048])
    x_own = din("x_own", [1024, 2048])
    w_in = din("w_in", [2048, 10320])
    w_pa = din("w_pa", [1024, 2048])
    w_pb = din("w_pb", [1024, 2048])
    w_out = din("w_out", [2048, 2048])
    w_gate = din("w_gate", [2048, DFF])
    w_up = din("w_up", [2048, DFF])
    w_down = din("w_down", [DFF, 2048])
    c_bf = din("c_bf", [128, 384])
    c_g1b = din("c_g1b", [128, 2048])
    c_g2b = din("c_g2b", [128, 2048])
    c_sm = din("c_sm", [128, 16])
    c_tril = din("c_tril", [128, 128])
    c_wspT = din("c_wspT", [128, 8, 128])
    c_bsp = din("c_bsp", [128, 1024])
    c_sgug = din("c_sgug", [128, 1024])
    c_rb = din("c_rb", [32, 8])
    c_rb31 = din("c_rb31", [32, 8])
    c_oh = din("c_oh", [32, 2, 512])
    c_ones32 = din("c_ones32", [32, 128])
    c_cneg = din("c_cneg", [128, 2, 256])
    y_own = nc.dram_tensor("y_own", [1024, 2048], F32, kind="ExternalOutput").ap()
    Fscr_t = nc.dram_tensor("fscr", [16 * 128 * 512], F32, kind="Internal")
    Fscr = Fscr_t.ap()
    dbg_out = {}

    with ExitStack() as es:
        ARENA_KB = 204
        arena_t = es.enter_context(nc.sbuf_tensor("arena", [128, ARENA_KB * KB // 2], BF16))
        arena = arena_t[:]
        psb = []
        for i in range(8):
            t = es.enter_context(nc.psum_tensor("ps%d" % i, [128, 512], F32))
            psb.append(t[:])
        PS = [Buf('ps%d' % i) for i in range(8)]
        sems = {}
        for e in ENG:
            sems[e] = es.enter_context(nc.semaphore("s_" + e))
        for qe in ('sp', 'pool'):
            for i in range(P.NDS):
                k = 'd_%s_%d' % (qe, i)
                sems[k] = es.enter_context(nc.semaphore(k))

        def R(off, nbytes, dt=BF16, pat=None, **kw):
            a = arena[:, off // 2:(off + nbytes) // 2]
            if dt != BF16:
                a = a.bitcast(dt)
            if pat:
                a = a.rearrange(pat, **kw)
            return a

        def psbf(i):
            return psb[i].bitcast(BF16)

        ident = R(0, 256)
        ones = R(256, 256)
        onesbd = R(512, 256)
        csm = R(768, 64, F32)
        st = R(1024, 256, F32)
        B_st = Buf('st')
        B_c = Buf('consts')
        GB = R(4 * KB, 8 * KB, F32)
        B_gb = Buf('gb')
        RING = [R((12 + 8 * i) * KB, 8 * KB, BF16, "p (c n) -> p c n", c=16) for i in range(3)]
        B_ring = [Buf('ring%d' % i) for i in range(3)]
        out_aT = R(36 * KB, 16 * KB, BF16, "p (h t) -> p h t", h=8)
        B_oa = Buf('out_aT')
        kT = R(52 * KB, 32 * KB, BF16, "p (h t) -> p h t", h=8)
        B_kT = Buf('kT')
        V = R(84 * KB, 32 * KB, BF16, "p (j d) -> p j d", j=16)
        B_V = Buf('V')
        kiT2 = R(116 * KB, 4 * KB)
        B_ki = Buf('kiT2')
        wtok = R(120 * KB, 512, F32, "p (t h) -> p t h", t=8)
        B_wt = Buf('wtok')
        qT = R(124 * KB, 16 * KB, BF16, "p (h t) -> p h t", h=8)
        B_qT = Buf('qT')
        qiT = R(140 * KB, 16 * KB, BF16, "p (h t) -> p h t", h=8)
        B_qi = Buf('qiT')
        hT = R(156 * KB, 32 * KB, BF16, "p (c t) -> p c t", c=16)
        B_hT = [Buf('hT0'), Buf('hT1')]

        def pick(B, idx):
            return B[idx] if isinstance(B, list) else B
        xbuf = R(188 * KB, 8 * KB, F32)
        B_xb = Buf('xbuf')
        xn = R(196 * KB, 4 * KB)
        B_xn = Buf('xn')
        sqb = R(200 * KB, 1 * KB)
        B_sq = Buf('sqb')
        rstd = R(201 * KB, 2 * KB, F32)
        B_rs = Buf('rstd')

        def wait_all(e, waits):
            for k, v in waits:
                e.wait_ge(sems[k], v)

        def dma_sp(out, in_, reads=(), writes=()):
            P.dma('sp', lambda e, o=out, i=in_: e.dma_start(out=o, in_=i), reads, writes)

        def dma_pool(out, in_, reads=(), writes=()):
            P.dma('pool', lambda e, o=out, i=in_: e.dma_start(out=o, in_=i), reads, writes)

        ring_seq = [0]

        def load_block(specs):
            si = ring_seq[0] % 3
            ring_seq[0] += 1
            slot, b = RING[si], B_ring[si]
            first = True
            for (src, c0, col0) in specs:
                rows, cols = src.shape
                nchunk = rows // 128
                o = slot[:, c0:c0 + nchunk, col0:col0 + cols]
                i = src.rearrange("(c p) n -> p c n", p=128)
                P.dma('pool', lambda e, o=o, i=i: e.dma_start(out=o, in_=i), (), (b,), append=not first)
                first = False
            return slot, b

        def pe_op(fn, reads, writes, ring_bufs=()):
            return P.op('pe', fn, tuple(reads) + tuple(ring_bufs), writes)

        def mm_acc(e, out, pairs):
            ins = None
            n = len(pairs)
            for i, (l, r) in enumerate(pairs):
                ins = e.matmul(out, l, r, start=(i == 0), stop=(i == n - 1))
            return ins

        tasks = []
        pending = []

        def flush_pending():
            while pending:
                pending.pop(0)()

        BIGR = [R(12 * KB, 12 * KB), R(24 * KB, 12 * KB)]
        B_big = [Buf('big0'), Buf('big1')]
        big_seq = [0]

        def load_big(arg):
            nch, specs = arg if isinstance(arg, tuple) else (16, arg)
            si = big_seq[0] % 2
            big_seq[0] += 1
            view = BIGR[si][:, 0:nch * 256].rearrange("p (c n) -> p c n", c=nch)
            b = B_big[si]
            first = True
            for (src, c0, col0) in specs:
                rows, cols = src.shape
                nchunk = rows // 128
                o = view[:, c0:c0 + nchunk, col0:col0 + cols]
                i = src.rearrange("(c p) n -> p c n", p=128)
                P.dma('pool', lambda e, o=o, i=i: e.dma_start(out=o, in_=i), (), (b,), append=not first)
                first = False
            return view, b

        def run_tasks(big=False):
            loaders = [i for i, t in enumerate(tasks) if t[0] is not None]
            loaded = {}
            li = 0
            la = 1 if big else 2
            lfn = load_big if big else load_block
            for i, (ld, comp) in enumerate(tasks):
                if ld is not None:
                    my = loaders.index(i)
                    while li < len(loaders) and li <= my + la:
                        loaded[loaders[li]] = lfn(tasks[loaders[li]][0]())
                        li += 1
                    if not getattr(comp, '_fm', False):
                        flush_pending()
                    comp(*loaded.pop(i))
                else:
                    flush_pending()
                    comp()
            flush_pending()
            tasks.clear()

        dma_pool(R(0, 768), c_bf, (), (B_c,))
        dma_sp(csm, c_sm, (), (B_c,))
        dma_sp(GB, c_g1b, (), (B_gb,))

        psrot = [0]

        def next_ps(lo=0, n=3):
            i = lo + psrot[0] % n
            psrot[0] += 1
            return i

        xsets = {
            'a': [(xbuf, xn, B_xb, B_xn), (R(36 * KB, 8 * KB, F32), R(44 * KB, 4 * KB), Buf(), Buf()),
                  (R(124 * KB, 8 * KB, F32), R(132 * KB, 4 * KB), Buf(), Buf()),
                  (R(136 * KB, 8 * KB, F32), R(144 * KB, 4 * KB), Buf(), Buf())],
            'b': [(xbuf, xn, B_xb, B_xn), (R(166 * KB, 8 * KB, F32), R(174 * KB, 4 * KB), Buf(), Buf()),
                  (R(100 * KB, 8 * KB, F32), R(108 * KB, 4 * KB), Buf(), Buf()),
                  (R(112 * KB, 8 * KB, F32), R(120 * KB, 4 * KB), Buf(), Buf())],
        }
        B_sth = [Buf(), Buf(), Buf(), Buf()]
        hcnt = [0]

        def make_hT(src_rows, ntiles, dst, B_dst, tok0=0, xs='a', resident=None, xnbufs=None, gcols=None):
            tiles = []

            def s1(tt):
                par = hcnt[0] % 4
                hcnt[0] += 1
                sc = st[:, 3 * par:3 * par + 3]
                Bs = B_sth[par]
                if resident is None:
                    xb_, xn_, Bxb, Bxn = xsets[xs][par]
                    dma_sp(xb_, src_rows(tt), (), (Bxb,))
                else:
                    xb_, Bxb = resident(tt)
                    xn_, Bxn = xnbufs[tt % len(xnbufs)]
                P.op('act', lambda e: e.activation(out=xn_, in_=xb_, func=AF.Square, accum_out=sc[:, 0:1]),
                     (Bxb,), (Bxn, Bs))
                P.op('act', lambda e: e.activation(out=sc[:, 1:2], in_=sc[:, 0:1], func=AF.Ln, scale=1.0 / D, bias=EPS),
                     (Bs,), (Bs,))
                P.op('act', lambda e: e.activation(out=sc[:, 2:3], in_=sc[:, 1:2], func=AF.Exp, scale=-0.5), (Bs,), (Bs,))
                P.op('dve', lambda e: e.scalar_tensor_tensor(out=xn_, in0=xb_, scalar=sc[:, 2:3], in1=GB,
                                                              op0=ALU.mult, op1=ALU.mult),
                     (Bxb, Bs, B_gb), (Bxn,))
                tiles.append((xn_, Bxn))

            def s2(tt):
                xn_, Bxn = tiles[tt]
                for half in range(2):
                    pi = 5 + half
                    pv = psbf(pi)

                    def tr(e, half=half, pv=pv):
                        ins = None
                        for c in range(8):
                            cc = half * 8 + c
                            ins = e.transpose(pv[:, c * 128:(c + 1) * 128], xn_[:, cc * 128:(cc + 1) * 128], ident)
                        return ins
                    P.op('pe', tr, (Bxn, B_c), (PS[pi],))
                    o = dst[:, half * 8:half * 8 + 8, tok0 + tt * 128: tok0 + (tt + 1) * 128]
                    i_ = pv.rearrange("p (c t) -> p c t", c=8)
                    if half == 0:
                        P.op('act', lambda e, o=o, i_=i_: e.activation(out=o, in_=i_, func=AF.Copy), (PS[pi],), (pick(B_dst, (tok0 + tt * 128) // 512),))
                    else:
                        P.op('dve', lambda e, o=o, i_=i_: e.tensor_copy(out=o, in_=i_), (PS[pi],), (pick(B_dst, (tok0 + tt * 128) // 512),))
            LA = 2 if resident is None else 1
            for i in range(ntiles + LA):
                if i < ntiles:
                    s1(i)
                if i >= LA:
                    s2(i - LA)

        nsets = [(sqb, rstd, B_sq, B_rs), (R(48 * KB, 1 * KB), R(49 * KB, 2 * KB, F32), Buf(), Buf())]
        ncnt = [0]

        def norm_post(pi, dst, B_dst, gcol, ones_ap, inv_n, npart=128):
            ps = psb[pi]
            sq_, rs_, Bsq, Brs = nsets[ncnt[0] % 2]
            p2 = 3 + (ncnt[0] % 2)
            ncnt[0] += 1
            P.op('act', lambda e: e.activation(out=sq_, in_=ps, func=AF.Square), (PS[pi],), (Bsq,))
            P.op('pe', lambda e: e.matmul(psb[p2], ones_ap, sq_, start=True, stop=True), (Bsq, B_c), (PS[p2],))
            P.op('act', lambda e: e.activation(out=rs_, in_=psb[p2], func=AF.Ln, scale=inv_n, bias=EPS),
                 (PS[p2],), (Brs,))
            P.op('act', lambda e: e.activation(out=rs_, in_=rs_, func=AF.Exp, scale=-0.5), (Brs,), (Brs,))
            P.op('dve', lambda e: e.scalar_tensor_tensor(out=dst, in0=ps, scalar=gcol, in1=rs_,
                                                          op0=ALU.mult, op1=ALU.mult),
                 (PS[pi], Brs, B_c), (B_dst,))

        def fm_block_task(wsrc, col0, consumer, src_hT=None, B_src=None, ntokchunks=2, tok0=0, ncc=2, K=16, M=128):
            src_hT = hT if src_hT is None else src_hT
            B_src = B_hT if B_src is None else B_src

            def ld():
                if K == 16:
                    return [(wsrc[0:1024, col0:col0 + ncc * M], 0, 0), (wsrc[1024:2048, col0:col0 + ncc * M], 8, 0)]
                return [(wsrc[0:1024, col0:col0 + ncc * M], 0, 0)]

            def comp(slot, b):
                for cc in range(ncc):
                    for n in range(ntokchunks):
                        pi = next_ps()
                        pairs = [(slot[:, c, cc * M:(cc + 1) * M], src_hT[:, c, tok0 + n * 512: tok0 + (n + 1) * 512])
                                 for c in range(K)]
                        pe_op(lambda e, pi=pi, pairs=pairs: mm_acc(e, psb[pi], pairs), (pick(B_src, (tok0 + n * 512) // 512),), (PS[pi],), (b,))
                        flush_pending()
                        pending.append(lambda cc=cc, n=n, pi=pi: consumer(cc, n, pi))
            comp._fm = True
            tasks.append((ld, comp))

        def tm_block_task(wsrc, col0, ncols, consumer, lhs, B_lhs, ntiles=8, K=16, row0=0):
            def ld():
                sp = []
                r = 0
                c0 = 0
                while r < K * 128:
                    nr = min(1024, K * 128 - r)
                    sp.append((wsrc[row0 + r:row0 + r + nr, col0:col0 + ncols], c0, 0))
                    r += nr
                    c0 += nr // 128
                return sp

            def comp(slot, b):
                flush_pending()
                for tt in range(ntiles):
                    pi = next_ps()
                    pairs = [(lhs(c, tt), slot[:, c, 0:ncols]) for c in range(K)]
                    pe_op(lambda e, pi=pi, pairs=pairs: mm_acc(e, psb[pi][:, 0:ncols], pairs), (pick(B_lhs, tt // 4),), (PS[pi],), (b,))
                    consumer(tt, pi)
            tasks.append((ld, comp))

        for half in range(2):
            tasks.append((None, lambda half=half: make_hT(
                lambda tt, half=half: x_all[half * 1024 + tt * 128: half * 1024 + (tt + 1) * 128, :], 8, hT, B_hT)))
            for blk in range(4):
                def cons_k(cc, n, pi, blk=blk, half=half):
                    head = 2 * blk + cc
                    t0 = half * 1024 + n * 512
                    norm_post(pi, kT[:, head, t0:t0 + 512], B_kT, csm[:, 1:2], ones, 1.0 / 128)
                fm_block_task(w_in, 1024 + 256 * blk, cons_k)
            for blk in range(4):
                def cons_v(tt, pi, blk=blk, half=half):
                    o = V[:, half * 8 + tt, 256 * blk:256 * blk + 256]
                    P.op('act', lambda e, o=o, pi=pi: e.activation(out=o, in_=psb[pi][:, 0:256], func=AF.Copy),
                         (PS[pi],), (B_V,))
                tm_block_task(w_in, 2048 + 256 * blk, 256, cons_v,
                              lambda c, tt: hT[:, c, tt * 128:(tt + 1) * 128], B_hT)

            def ld_ki():
                sp = []
                for j in range(2):
                    for dup in range(2):
                        sp.append((w_in[j * 1024:(j + 1) * 1024, 4096:4160], 8 * j, 64 * dup))
                return sp

            def comp_ki(slot, b, half=half):
                for n in range(2):
                    pi = next_ps()
                    pairs = [(slot[:, c, 0:128], hT[:, c, n * 512:(n + 1) * 512]) for c in range(16)]
                    pe_op(lambda e, pi=pi, pairs=pairs: mm_acc(e, psb[pi], pairs), (pick(B_hT, n),), (PS[pi],), (b,))
                    t0 = half * 1024 + n * 512
                    norm_post(pi, kiT2[:, t0:t0 + 512], B_ki, csm[:, 2:3], onesbd, 1.0 / 64)
            tasks.append((ld_ki, comp_ki))

        tasks.append((None, lambda: make_hT(lambda tt: x_own[tt * 128:(tt + 1) * 128, :], 8, hT, B_hT)))
        for blk in range(4):
            def cons_q(cc, n, pi, blk=blk):
                head = 2 * blk + cc
                norm_post(pi, qT[:, head, n * 512:(n + 1) * 512], B_qT, csm[:, 0:1], ones, 1.0 / 128)
            fm_block_task(w_in, 256 * blk, cons_q)
        for blk in range(4):
            def cons_qi(cc, n, pi, blk=blk):
                o = qiT[:, 2 * blk + cc, n * 512:(n + 1) * 512]
                P.op('act', lambda e, o=o, pi=pi: e.activation(out=o, in_=psb[pi], func=AF.Copy), (PS[pi],), (B_qi,))
            fm_block_task(w_in, 3072 + 256 * blk, cons_qi)

        def cons_wi(tt, pi):
            P.op('dve', lambda e, tt=tt, pi=pi: e.tensor_copy(out=wtok[:, tt, :], in_=psb[pi][:, 0:16]), (PS[pi],), (B_wt,))
        tm_block_task(w_in, 4160, 16, cons_wi, lambda c, tt: hT[:, c, tt * 128:(tt + 1) * 128], B_hT)

        run_tasks()
        P.barrier()

        if dbg and stage == 2:
            for nm, ap_, shp in (("kT", kT, [128, 8, 2048]), ("V", V, [128, 16, 1024]), ("qT", qT, [128, 8, 1024]),
                                 ("qiT", qiT, [128, 8, 1024])):
                d = nc.dram_tensor("dbg_" + nm, shp, BF16, kind="ExternalOutput").ap()
                dma_sp(d, ap_)
            d = nc.dram_tensor("dbg_kiT2", [128, 2048], BF16, kind="ExternalOutput").ap()
            dma_sp(d, kiT2)
            d = nc.dram_tensor("dbg_wtok", [128, 8, 16], F32, kind="ExternalOutput").ap()
            dma_sp(d, wtok)

        if stage >= 3:
            accs = [R(156 * KB, 8 * KB, F32), R(164 * KB, 8 * KB, F32)]
            m01s = [R(172 * KB, 4 * KB), R(176 * KB, 4 * KB)]
            maskT = R(180 * KB, 4 * KB, BF16, "p (j t) -> p j t", j=16)
            MW = R(184 * KB, 6 * KB, BF16, "p (h b t) -> p h b t", h=8, b=3)
            EBW = R(190 * KB, 12 * KB, BF16, "p (q h b t) -> p q h b t", q=2, h=8, b=3)
            rden = R(202 * KB, 512, F32)
            Dg = [R(12 * KB, 4 * KB, BF16, "p (h t) -> p h t", h=16), R(16 * KB, 4 * KB, BF16, "p (h t) -> p h t", h=16)]
            Rt = [R(20 * KB, 1 * KB), R(21 * KB, 1 * KB), R(22 * KB, 1 * KB)]
            PT = [R(23 * KB, 1 * KB), R(24 * KB, 1 * KB), R(203 * KB, 1 * KB)]
            cneg = R(25 * KB, 2 * KB, F32, "p (q s) -> p q s", q=2)
            sgnw = R(27 * KB, 512, F32, "p (t h) -> p t h", t=8)
            absw = R(27 * KB + 512, 512, F32, "p (t h) -> p t h", t=8)
            B_ebw, B_cn, B_mT, B_mw, B_rd, B_sw = (Buf() for _ in range(6))
            B_acc = [Buf(), Buf()]
            B_m01 = [Buf(), Buf()]
            B_dg = [Buf(), Buf()]
            B_rt = [Buf(), Buf(), Buf()]
            B_pt = [Buf(), Buf(), Buf()]
            B_sb = [Buf(), Buf()]
            B_den = [Buf(), Buf()]
            oh = R(28 * KB, 4 * KB, F32, "p (q u) -> p q u", q=2)[0:32]
            Lh = R(32 * KB, 4 * KB, F32, "p (h m) -> p h m", h=8)[0:32]
            rb = R(164 * KB, 32, F32)[0:32, :]
            rb31 = R(164 * KB + 32, 32, F32)[0:32, :]
            ones32 = R(164 * KB + 64, 512, F32)[0:32]
            vld = R(165 * KB, 2 * KB, F32)
            e1 = R(167 * KB, 2 * KB, F32)
            stage_t = R(169 * KB, 1536, F32)
            B_eb = Buf()
            B_e1 = Buf()
            B_stg = Buf()
            dma_sp(rb, c_rb, (), (B_eb,))
            dma_sp(rb31, c_rb31, (), (B_eb,))
            dma_sp(oh, c_oh, (), (B_eb,))
            dma_sp(ones32, c_ones32, (), (B_eb,))
            dma_sp(cneg, c_cneg, (), (B_cn,))
            P.op('dve', lambda e: e.tensor_tensor(out=rb, in0=rb, in1=rb31, op=ALU.subtract), (B_eb,), (B_eb,))
            for h in range(8):
                P.op('dve', lambda e, h=h: e.tensor_scalar(out=Lh[:, h, :], in0=ones32, scalar1=rb[:, h:h + 1], scalar2=None,
                                                            op0=ALU.mult), (B_eb,), (B_eb,))
            P.op('act', lambda e: e.activation(out=sgnw, in_=wtok, func=AF.Sign), (B_wt,), (B_sw,))
            P.op('dve', lambda e: e.tensor_tensor(out=absw, in0=wtok, in1=sgnw, op=ALU.mult), (B_wt, B_sw), (B_sw,))
            e1s = [R((156 + 2 * i) * KB, 2 * KB, F32) for i in range(8)]
            stgs = [R(172 * KB + 1536 * i, 1536, F32) for i in range(8)]
            B_e1s = [Buf() for _ in range(8)]
            B_stgs = [Buf() for _ in range(8)]
            vlds = [R(185 * KB, 2 * KB, F32), R(187 * KB, 2 * KB, F32)]
            B_vld = [Buf(), Buf()]
            for q in range(2):
                P.op('pe', lambda e, q=q: e.matmul(psb[0], ones32, oh[:, q, :], start=True, stop=True), (B_eb,), (PS[0],))
                P.op('dve', lambda e, q=q: e.tensor_copy(out=vlds[q], in_=psb[0]), (PS[0],), (B_vld[q],))
            for q in range(2):
                for h in range(8):
                    i8 = h
                    pi = 1 + (h % 3)
                    e1 = e1s[i8]
                    P.op('pe', lambda e, q=q, h=h, pi=pi: e.matmul(psb[pi], Lh[:, h, :], oh[:, q, :], start=True, stop=True),
                         (B_eb,), (PS[pi],))
                    P.op('act', lambda e, pi=pi, e1=e1: e.activation(out=e1, in_=psb[pi], func=AF.Exp), (PS[pi],), (B_e1s[i8],))
                    P.op('dve', lambda e, e1=e1, q=q: e.tensor_tensor(out=e1, in0=e1, in1=vlds[q], op=ALU.mult),
                         (B_e1s[i8], B_vld[q]), (B_e1s[i8],))
                    base = (h * 2 + q) * 65536
                    dma_sp(Fscr[base:base + 65536].rearrange("(r u) -> r u", u=512), e1, (B_e1s[i8],), (B_stgs[i8],))
                for h in range(8):
                    i8 = h
                    base = (h * 2 + q) * 65536
                    src = Fscr[base + 127: base + 127 + 128 * 511].rearrange("(s c) -> s c", c=511)[:, 0:384]
                    dma_sp(stgs[i8], src, (B_stgs[i8],), (B_stgs[i8],))
                    for b3 in range(3):
                        P.op('dve', lambda e, q=q, h=h, b3=b3, i8=i8: e.tensor_copy(
                            out=EBW[:, q, h, b3, :], in_=stgs[i8][:, (2 - b3) * 128:(3 - b3) * 128]), (B_stgs[i8],), (B_ebw,))
            P.barrier()

            SCALE = 128.0 ** -0.5
            SB0 = 24

            def gen_A(k):
                th = []
                kp = k % 2
                nk = 2 * k + 2
                Sk = 128 * nk
                items = [(c0, min(512, Sk - c0), h) for c0 in range(0, Sk, 512) for h in range(16)]

                def x_op(i):
                    c0, wc, h = items[i]
                    pi = i % 2
                    pr = (h % 2) * 64
                    P.op('pe', lambda e: e.matmul(
                        psb[pi][:, 0:wc], qiT[pr:pr + 64, h // 2, k * 128:(k + 1) * 128], kiT2[pr:pr + 64, c0:c0 + wc],
                        start=True, stop=True), (B_qi, B_ki), (PS[pi],))

                def mid_op(i):
                    c0, wc, h = items[i]
                    pi = i % 2
                    ri = i % 3
                    P.op('act', lambda e: e.activation(out=Rt[ri][:, 0:wc], in_=psb[pi][:, 0:wc], func=AF.Relu,
                                                       scale=absw[:, k, h:h + 1]), (PS[pi], B_sw), (B_rt[ri],))

                def fin_op(i):
                    c0, wc, h = items[i]
                    ri = i % 3
                    P.op('pe', lambda e: e.matmul(psb[7][:, 0:wc], Dg[kp][:, h, :], Rt[ri][:, 0:wc],
                                                  start=(h == 0), stop=(h == 15)), (B_dg[kp], B_rt[ri]), (PS[7],))
                    if h == 15:
                        P.op('dve', lambda e: e.tensor_copy(out=accs[kp][:, c0:c0 + wc], in_=psb[7][:, 0:wc]),
                             (PS[7],), (B_acc[kp],))
                for i in range(len(items) + 1):
                    def unit_a(i=i):
                        if i < len(items):
                            x_op(i)
                        if i >= 1:
                            mid_op(i - 1)

                    def unit_b(i=i):
                        if i >= 1:
                            fin_op(i - 1)
                    th.append(unit_a)
                    th.append(unit_b)
                return th

            def gen_D(k):
                kp = k % 2
                return [lambda: P.op('dve', lambda e: e.tensor_tensor(
                    out=Dg[kp], in0=ident.unsqueeze(1).broadcast_to([128, 16, 128]),
                    in1=sgnw[:, k, :].unsqueeze(2).broadcast_to([128, 16, 128]), op=ALU.mult), (B_c, B_sw), (B_dg[kp],))]

            def gen_B(k):
                th = []
                kp = k % 2
                q = k % 2
                nk = 2 * k + 2
                Sk = 128 * nk
                acc = accs[kp]
                m01 = m01s[kp]
                s0 = SB0 + 8 * kp
                cB, cW, cLo, cMid, cCnt, cG = (st[:, s0 + i:s0 + i + 1] for i in range(6))
                Bs = B_sb[kp]
                th.append(lambda: P.op('dve', lambda e: e.tensor_reduce(out=cB, in_=acc[:, 0:Sk], axis=AX.X, op=ALU.max,
                                                                       apply_absolute_value=True), (B_acc[kp],), (Bs,)))
                th.append(lambda: P.op('dve', lambda e: e.tensor_tensor(out=acc[:, Sk - 256:Sk], in0=acc[:, Sk - 256:Sk],
                                                                       in1=cneg[:, q, :], op=ALU.add), (B_acc[kp], B_cn), (B_acc[kp],)))
                th.append(lambda: P.op('dve', lambda e: e.tensor_scalar(out=cW, in0=cB, scalar1=2.0, scalar2=2.0,
                                                                       op0=ALU.mult, op1=ALU.add), (Bs,), (Bs,)))
                th.append(lambda: P.op('dve', lambda e: e.tensor_scalar(out=cLo, in0=cB, scalar1=-1.0, scalar2=-1.0,
                                                                       op0=ALU.mult, op1=ALU.add), (Bs,), (Bs,)))
                for it in range(NIT):
                    f = 2.0 ** -(it + 1)
                    th.append(lambda f=f: P.op('dve', lambda e: e.scalar_tensor_tensor(out=cMid, in0=cW, scalar=f, in1=cLo,
                                                                                       op0=ALU.mult, op1=ALU.add), (Bs,), (Bs,)))
                    th.append(lambda: P.op('dve', lambda e: e.tensor_scalar(out=m01[:, 0:Sk], in0=acc[:, 0:Sk], scalar1=cMid,
                                                                           scalar2=None, op0=ALU.is_ge, op1=ALU.add, accum_out=cCnt),
                                           (B_acc[kp], Bs), (B_m01[kp], Bs)))
                    th.append(lambda f=f: P.op('dve', lambda e: e.tensor_scalar(out=cG, in0=cCnt, scalar1=255.5, scalar2=f,
                                                                               op0=ALU.is_ge, op1=ALU.mult), (Bs,), (Bs,)))
                    th.append(lambda: P.op('dve', lambda e: e.scalar_tensor_tensor(out=cLo, in0=cG, scalar=cW, in1=cLo,
                                                                                   op0=ALU.mult, op1=ALU.add), (Bs,), (Bs,)))
                th.append(lambda: P.op('dve', lambda e: e.tensor_scalar(out=m01[:, 0:Sk], in0=acc[:, 0:Sk], scalar1=cLo,
                                                                       scalar2=None, op0=ALU.is_ge), (B_acc[kp], Bs), (B_m01[kp],)))
                return th

            def gen_C(k):
                th = []
                kp = k % 2
                q = k % 2
                nk = 2 * k + 2
                m01 = m01s[kp]
                for j0 in range(0, nk, 8):
                    j1 = min(nk, j0 + 8)

                    def trunit(j0=j0, j1=j1):
                        pv = psbf(6)

                        def trm(e):
                            ins = None
                            for j in range(j0, j1):
                                ins = e.transpose(pv[:, (j - j0) * 128:(j - j0 + 1) * 128], m01[:, j * 128:(j + 1) * 128], ident)
                            return ins
                        P.op('pe', trm, (B_m01[kp], B_c), (PS[6], B_den[0], B_den[1]))
                        P.op('act', lambda e: e.activation(
                            out=maskT[:, j0:j1, :], in_=pv[:, 0:(j1 - j0) * 128].rearrange("p (j t) -> p j t", t=128),
                            func=AF.Copy), (PS[6], B_den[0], B_den[1]), (B_mT,))
                    th.append(trunit)
                b_lo = 1 if k == 0 else 0
                jw0 = 2 * k - 1 + b_lo
                nb3 = 3 - b_lo
                th.append(lambda: P.op('dve', lambda e: e.tensor_tensor(
                    out=MW[:, :, b_lo:3, :], in0=maskT[:, jw0:nk, :].unsqueeze(1).broadcast_to([128, 8, nb3, 128]),
                    in1=EBW[:, q, :, b_lo:3, :], op=ALU.mult), (B_mT, B_ebw), (B_mw,)))
                npro = len(th)
                items = [(h, g0) for h in range(8) for g0 in range(0, nk, 4)]

                def qk_op(i):
                    h, g0 = items[i]
                    g1 = min(nk, g0 + 4)
                    pl = 2 + (i % 2)

                    def qk(e):
                        ins = None
                        for j in range(g0, g1):
                            ins = e.matmul(psb[pl][:, (j - g0) * 128:(j - g0 + 1) * 128], kT[:, h, j * 128:(j + 1) * 128],
                                           qT[:, h, k * 128:(k + 1) * 128], start=True, stop=True)
                        return ins
                    P.op('pe', qk, (B_kT, B_qT), (PS[pl],))

                def mid_op(i):
                    h, g0 = items[i]
                    g1 = min(nk, g0 + 4)
                    wg = (g1 - g0) * 128
                    pl = 2 + (i % 2)
                    pti = i % 3
                    P.op('act', lambda e: e.activation(
                        out=PT[pti][:, 0:wg], in_=psb[pl][:, 0:wg], func=AF.Exp, scale=SCALE, bias=csm[:, 4 + h:5 + h]),
                        (PS[pl], B_c), (B_pt[pti],))
                    fa1 = min(g1, jw0)
                    if fa1 > g0:
                        wf = (fa1 - g0) * 128
                        P.op('dve', lambda e: e.tensor_tensor(
                            out=PT[pti][:, 0:wf].rearrange("p (j t) -> p j t", t=128),
                            in0=PT[pti][:, 0:wf].rearrange("p (j t) -> p j t", t=128),
                            in1=maskT[:, g0:fa1, :], op=ALU.mult), (B_pt[pti], B_mT), (B_pt[pti],))
                    wa0 = max(g0, jw0)
                    if g1 > wa0:
                        o0 = (wa0 - g0) * 128
                        bb0 = wa0 - (2 * k - 1)
                        bb1 = g1 - (2 * k - 1)
                        P.op('dve', lambda e: e.tensor_tensor(
                            out=PT[pti][:, o0:wg].rearrange("p (j t) -> p j t", t=128),
                            in0=PT[pti][:, o0:wg].rearrange("p (j t) -> p j t", t=128),
                            in1=MW[:, h, bb0:bb1, :], op=ALU.mult), (B_pt[pti], B_mw), (B_pt[pti],))

                def fin_op(i):
                    h, g0 = items[i]
                    g1 = min(nk, g0 + 4)
                    pti = i % 3
                    po = 4 + (h % 2)
                    hh = h % 2

                    def pv_(e):
                        ins = None
                        for j in range(g0, g1):
                            pj = PT[pti][:, (j - g0) * 128:(j - g0 + 1) * 128]
                            e.matmul(psb[po][:, 0:128], V[:, j, h * 128:(h + 1) * 128], pj, start=(j == 0), stop=(j == nk - 1))
                            ins = e.matmul(psb[6][:, hh * 128:hh * 128 + 128], ones, pj, start=(j == 0), stop=(j == nk - 1))
                        return ins
                    P.op('pe', pv_, (B_V, B_pt[pti], B_c), (PS[po], B_den[hh]))
                    if g1 == nk:
                        P.op('act', lambda e: e.activation(out=rden, in_=psb[6][:, hh * 128:hh * 128 + 128], func=AF.Ln), (B_den[hh],), (B_rd,))
                        P.op('act', lambda e: e.activation(out=rden, in_=rden, func=AF.Exp, scale=-1.0), (B_rd,), (B_rd,))
                        P.op('dve', lambda e: e.tensor_tensor(
                            out=out_aT[:, h, k * 128:(k + 1) * 128], in0=psb[po][:, 0:128], in1=rden, op=ALU.mult),
                            (PS[po], B_rd), (B_oa,))
                for i in range(len(items) + 1):
                    def unit_a(i=i):
                        if i < len(items):
                            qk_op(i)
                        if i >= 1:
                            mid_op(i - 1)

                    def unit_b(i=i):
                        if i >= 1:
                            fin_op(i - 1)
                    th.append(unit_a)
                    th.append(unit_b)
                return th[:npro], th[npro:]

            def merge_run(lists):
                lists = [l for l in lists if l]
                pos = [0] * len(lists)
                while True:
                    best = -1
                    bf = 2.0
                    for i, l in enumerate(lists):
                        if pos[i] < len(l):
                            fr = pos[i] / len(l)
                            if fr < bf:
                                bf = fr
                                best = i
                    if best < 0:
                        break
                    lists[best][pos[best]]()
                    pos[best] += 1

            for r_ in range(-3, 8):
                ls = []
                if 0 <= r_ < 8:
                    pro, main = gen_C(r_)
                    for t_ in pro:
                        t_()
                    ls.append(main)
                if 0 <= r_ + 3 < 8:
                    ls.append(gen_D(r_ + 3))
                if 0 <= r_ + 2 < 8:
                    ls.append(gen_A(r_ + 2))
                if 0 <= r_ + 1 < 8:
                    ls.append(gen_B(r_ + 1))
                merge_run(ls)
            P.barrier()

        if dbg and stage == 3:
            d = nc.dram_tensor("dbg_oaT", [128, 8, 1024], BF16, kind="ExternalOutput").ap()
            dma_sp(d, out_aT)

        if stage >= 4:
            hT = R(52 * KB, 32 * KB, BF16, "p (c t) -> p c t", c=16)
            uT = R(84 * KB, 16 * KB, BF16, "p (g t) -> p g t", g=8)
            gv = R(100 * KB, 32 * KB, F32, "p (t d) -> p t d", t=8)
            vn = R(132 * KB, 16 * KB, BF16, "p (t d) -> p t d", t=8)
            wspT = R(148 * KB, 2 * KB, BF16, "p (g t) -> p g t", g=8)
            bspb = R(150 * KB, 4 * KB, F32)
            sgug = R(154 * KB, 4 * KB, F32)
            wspf = R(158 * KB, 4 * KB, F32, "p (g t) -> p g t", g=8)
            tril = R(162 * KB, 512, F32)
            ztmp = R(163 * KB, 2 * KB, F32)
            sa = R(166 * KB, 8 * KB, F32, "p (a n t) -> p a n t", a=2, n=2)
            sb = R(174 * KB, 8 * KB, F32, "p (a n t) -> p a n t", a=2, n=2)
            m1 = R(182 * KB, 6 * KB, F32)
            m1t = [R(182 * KB, 2 * KB, F32), R(184 * KB, 2 * KB, F32), R(186 * KB, 2 * KB, F32), R(201 * KB, 2 * KB, F32)]
            mergedT = R(100 * KB, 32 * KB, BF16, "p (c t) -> p c t", c=16)
            B_hT = [Buf(), Buf()]
            B_uT, B_gv, B_vn, B_wsp, B_zt, B_sa, B_sb, B_m1, B_mg = (Buf() for _ in range(9))
            B_mg = B_gv
            x1 = R(132 * KB, 64 * KB, F32, "p (t d) -> p t d", t=8)
            B_x1 = [Buf() for _ in range(8)]

            def ld_x1_early():
                for tt, extra in ((0, (B_vn,)), (1, (B_vn,)), (2, (B_wsp,)), (3, (B_wsp, B_zt))):
                    dma_sp(x1[:, tt, :], x_own[tt * 128:(tt + 1) * 128, :], (), (B_x1[tt],) + extra)
            dma_sp(wspf, c_wspT, (), (B_wsp,))
            dma_sp(tril, c_tril, (), (B_wsp,))
            dma_sp(bspb, c_bsp, (), (B_wsp,))
            dma_sp(sgug, c_sgug, (), (B_wsp,))
            for g in range(8):
                P.op('dve', lambda e, g=g: e.tensor_tensor(out=wspT[:, g, :], in0=wspf[:, g, :], in1=tril, op=ALU.mult),
                     (B_wsp,), (B_wsp,))
            tasks.append((None, lambda: make_hT(lambda tt: x_own[tt * 128:(tt + 1) * 128, :], 8, hT, B_hT, xs='b')))
            for blk in range(4):
                def cons_u(cc, n, pi, blk=blk):
                    o = uT[:, 2 * blk + cc, n * 512:(n + 1) * 512]
                    P.op('act', lambda e, o=o, pi=pi: e.activation(out=o, in_=psb[pi], func=AF.Gelu_apprx_tanh),
                         (PS[pi],), (B_uT,))
                fm_block_task(w_in, 4176 + 256 * blk, cons_u, src_hT=hT, B_src=B_hT)
            for blk in range(4):
                def cons_vb(tt, pi, blk=blk):
                    o = gv[:, tt, 256 * blk:256 * blk + 256]
                    P.op('act', lambda e, o=o, pi=pi: e.activation(out=o, in_=psb[pi][:, 0:256], func=AF.Gelu_apprx_tanh),
                         (PS[pi],), (B_gv,))
                tm_block_task(w_in, 5200 + 256 * blk, 256, cons_vb, lambda c, tt: hT[:, c, tt * 128:(tt + 1) * 128], B_hT)

            def sgu_core():
                for tt in range(8):
                    P.op('act', lambda e, tt=tt: e.activation(out=vn[:, tt, :], in_=gv[:, tt, :], func=AF.Square,
                                                              accum_out=st[:, 16:17]), (B_gv,), (B_vn, B_st))
                    P.op('act', lambda e: e.activation(out=st[:, 17:18], in_=st[:, 16:17], func=AF.Ln, scale=1.0 / 1024,
                                                       bias=EPS), (B_st,), (B_st,))
                    P.op('act', lambda e: e.activation(out=st[:, 18:19], in_=st[:, 17:18], func=AF.Exp, scale=-0.5), (B_st,), (B_st,))
                    P.op('dve', lambda e, tt=tt: e.scalar_tensor_tensor(out=vn[:, tt, :], in0=gv[:, tt, :], scalar=st[:, 18:19],
                                                                         in1=sgug, op0=ALU.mult, op1=ALU.mult),
                         (B_gv, B_st, B_wsp), (B_vn,))
                    for gh in range(2):
                        pi = 5 + gh

                        def zmm(e, tt=tt, gh=gh, pi=pi):
                            ins = None
                            for g4 in range(4):
                                g = gh * 4 + g4
                                ins = e.matmul(psb[pi][:, g4 * 128:(g4 + 1) * 128], vn[:, tt, g * 128:(g + 1) * 128],
                                               wspT[:, g, :], start=True, stop=True)
                            return ins
                        P.op('pe', zmm, (B_vn, B_wsp), (PS[pi],))
                        P.op('dve', lambda e, gh=gh, pi=pi: e.tensor_tensor(out=ztmp, in0=psb[pi], in1=bspb[:, gh * 512:(gh + 1) * 512],
                                                                            op=ALU.add), (PS[pi], B_wsp), (B_zt,))
                        P.op('dve', lambda e, gh=gh, tt=tt: e.tensor_tensor(
                            out=uT[:, gh * 4:gh * 4 + 4, tt * 128:(tt + 1) * 128],
                            in0=ztmp.rearrange("p (g t) -> p g t", g=4),
                            in1=uT[:, gh * 4:gh * 4 + 4, tt * 128:(tt + 1) * 128], op=ALU.mult), (B_zt, B_uT), (B_uT,))
            for c2 in range(8):
                def cons_ga(cc, n, pi):
                    P.op('act', lambda e, cc=cc, n=n, pi=pi: e.activation(out=sa[:, cc, n, :], in_=psb[pi], func=AF.Sigmoid),
                         (PS[pi],), (B_sa,))
                fm_block_task(w_in, 6224 + 256 * c2, cons_ga, src_hT=hT, B_src=B_hT)

                def cons_gb(cc, n, pi):
                    P.op('act', lambda e, cc=cc, n=n, pi=pi: e.activation(out=sb[:, cc, n, :], in_=psb[pi], func=AF.Sigmoid),
                         (PS[pi],), (B_sb,))
                fm_block_task(w_in, 8272 + 256 * c2, cons_gb, src_hT=hT, B_src=B_hT)
                if c2 == 0:
                    tasks.append((None, sgu_core))
                    if stage >= 5:
                        tasks.append((None, ld_x1_early))

                def cons_pa(cc, n, pi):
                    P.op('dve', lambda e, cc=cc, n=n, pi=pi: e.tensor_tensor(out=m1t[cc * 2 + n], in0=psb[pi], in1=sa[:, cc, n, :],
                                                                             op=ALU.mult), (PS[pi], B_sa), (B_m1,))
                fm_block_task(w_pa, 256 * c2, cons_pa, src_hT=out_aT, B_src=B_oa, K=8)

                def cons_pb(cc, n, pi, c2=c2):
                    P.op('dve', lambda e, cc=cc, n=n, pi=pi: e.tensor_tensor(out=sb[:, cc, n, :], in0=psb[pi], in1=sb[:, cc, n, :],
                                                                             op=ALU.mult), (PS[pi], B_sb), (B_sb,))
                    P.op('dve', lambda e, cc=cc, n=n, c2=c2: e.tensor_tensor(
                        out=mergedT[:, 2 * c2 + cc, n * 512:(n + 1) * 512], in0=sb[:, cc, n, :], in1=m1t[cc * 2 + n], op=ALU.add),
                        (B_sb, B_m1), (B_mg,))
                fm_block_task(w_pb, 256 * c2, cons_pb, src_hT=uT, B_src=B_uT, K=8)
            if stage < 5:
                run_tasks()
                P.barrier()
            else:
                tasks.append((None, P.barrier))

        if dbg and stage == 4:
            d = nc.dram_tensor("dbg_mg", [128, 16, 1024], BF16, kind="ExternalOutput").ap()
            dma_sp(d, mergedT)
            d = nc.dram_tensor("dbg_obT", [128, 8, 1024], BF16, kind="ExternalOutput").ap()
            dma_sp(d, uT)

        if stage >= 5:
            def ld_x1_late():
                for tt in range(4, 8):
                    dma_sp(x1[:, tt, :], x_own[tt * 128:(tt + 1) * 128, :], (), (B_x1[tt],))
                if stage >= 6:
                    dma_sp(GB, c_g2b, (), (B_gb,))
            tasks.append((None, ld_x1_late))
            for blk in range(8):
                def cons_o(tt, pi, blk=blk):
                    o = x1[:, tt, 256 * blk:256 * blk + 256]
                    P.op('dve', lambda e, o=o, pi=pi: e.tensor_tensor(out=o, in0=o, in1=psb[pi][:, 0:256], op=ALU.add),
                         (PS[pi], B_x1[tt]), (B_x1[tt],))
                tm_block_task(w_out, 256 * blk, 256, cons_o, lambda c, tt: mergedT[:, c, tt * 128:(tt + 1) * 128], B_mg)
            if stage < 6:
                run_tasks()
                P.barrier()

        if stage >= 6:
            h2T = R(36 * KB, 32 * KB, BF16, "p (c t) -> p c t", c=16)
            aT = R(68 * KB, 44 * KB, BF16, "p (f t) -> p f t", f=22)
            xn2 = R(196 * KB, 4 * KB)
            xn2b = R(200 * KB, 4 * KB)
            sg = [R((112 + 2 * i) * KB, 2 * KB, F32) for i in range(4)]
            B_h2 = [Buf(), Buf()]
            B_aT, B_xn2, B_xn2b = Buf(), Buf(), Buf()
            B_sg = [Buf() for _ in range(4)]
            tasks.append((None, lambda: make_hT(None, 8, h2T, B_h2, resident=lambda tt: (x1[:, tt, :], B_x1[tt]),
                                                xnbufs=[(xn2, B_xn2), (xn2b, B_xn2b)])))
            for fp in range(2):
                for f2l in range(11):
                    f2 = fp * 11 + f2l

                    def cons_g(cc, n, pi):
                        P.op('act', lambda e, cc=cc, n=n, pi=pi: e.activation(out=sg[cc * 2 + n], in_=psb[pi], func=AF.Silu),
                             (PS[pi],), (B_sg[cc * 2 + n],))
                    fm_block_task(w_gate, 256 * f2, cons_g, src_hT=h2T, B_src=B_h2)

                    def cons_up(cc, n, pi, f2l=f2l):
                        P.op('dve', lambda e, cc=cc, n=n, pi=pi, f2l=f2l: e.tensor_tensor(
                            out=aT[:, 2 * f2l + cc, n * 512:(n + 1) * 512], in0=psb[pi], in1=sg[cc * 2 + n], op=ALU.mult),
                            (PS[pi], B_sg[cc * 2 + n]), (B_aT,))
                    fm_block_task(w_up, 256 * f2, cons_up, src_hT=h2T, B_src=B_h2)
                for dblk in range(8):
                    for rbk in range(2):
                        f0 = 16 * rbk
                        nf = 16 if rbk == 0 else 6

                        def ld(dblk=dblk, fp=fp, f0=f0, nf=nf):
                            sp = []
                            r0 = (fp * 22 + f0) * 128
                            r = 0
                            while r < nf * 128:
                                nr = min(1024, nf * 128 - r)
                                sp.append((w_down[r0 + r:r0 + r + nr, 256 * dblk:256 * dblk + 256], r // 128, 0))
                                r += nr
                            return sp

                        def comp(slot, b, dblk=dblk, fp=fp, f0=f0, nf=nf, rbk=rbk):
                            for tt in range(8):
                                def dmm(e, tt=tt):
                                    ins = None
                                    for c in range(nf):
                                        f = f0 + c
                                        ins = e.matmul(psb[tt][:, 0:256], aT[:, f, tt * 128:(tt + 1) * 128], slot[:, c, 0:256],
                                                       start=(f == 0), stop=(f == 21))
                                    return ins
                                pe_op(dmm, (B_aT,), (PS[tt],), (b,))
                                if rbk == 1:
                                    o = x1[:, tt, 256 * dblk:256 * dblk + 256]
                                    P.op('dve', lambda e, o=o, tt=tt: e.tensor_tensor(out=o, in0=o, in1=psb[tt][:, 0:256], op=ALU.add),
                                         (PS[tt], B_x1[tt]), (B_x1[tt],))
                                    if fp == 1 and dblk == 7:
                                        dma_sp(y_own[tt * 128:(tt + 1) * 128, :], x1[:, tt, :], (B_x1[tt],), ())
                        tasks.append((ld, comp))
            run_tasks()
        elif not dbg:
            raise RuntimeError("no output stage")
        P.barrier()

        with nc.Block() as block:
            def run(name, e):
                for waits, fn, inc in P.q[name]:
                    wait_all(e, waits)
                    if fn is None:
                        continue
                    ins = fn(e)
                    ins.then_inc(sems[inc[0]], inc[1])

            @block.tensor
            def _(e):
                run('pe', e)

            @block.scalar
            def _(e):
                run('act', e)

            @block.vector
            def _(e):
                run('dve', e)

            @block.gpsimd
            def _(e):
                run('pool', e)

            @block.sync
            def _(e):
                run('sp', e)
    return nc


def _bucket(d):
    d = np.asarray(d)
    df = np.maximum(d.astype(np.float32), np.float32(1.0))
    large = 16 + (np.log(df / np.float32(16)) / np.float32(math.log(8.0)) * np.float32(16)).astype(np.int32)
    large = np.minimum(large, 31)
    return np.where(d < 16, d, large)


def host_consts(r, inputs):
    c = {}
    ident = np.eye(128, dtype=np.float32)
    ones = np.ones((128, 128), np.float32)
    bd = np.zeros((128, 128), np.float32)
    bd[:64, :64] = 1
    bd[64:, 64:] = 1
    c["c_bf"] = np.concatenate([ident, ones, bd], axis=1)
    c["c_g1b"] = np.ascontiguousarray(np.broadcast_to(inputs["norm1_g"][0][None, :], (128, 2048)))
    c["c_g2b"] = np.ascontiguousarray(np.broadcast_to(inputs["norm2_g"][0][None, :], (128, 2048)))
    sm = np.zeros((128, 16), np.float32)
    sm[:, 0] = inputs["q_norm_g"][0]
    sm[:, 1] = inputs["k_norm_g"][0]
    sm[:, 2] = np.concatenate([inputs["idx_k_norm_g"][0]] * 2)
    sm[:, 4:12] = inputs["rel_bias"][31][None, :]
    c["c_sm"] = sm
    s_ = np.arange(128)
    c["c_tril"] = (s_[:, None] <= s_[None, :]).astype(np.float32)
    c["c_wspT"] = np.ascontiguousarray(inputs["w_spatial"][0].transpose(2, 0, 1))
    c["c_bsp"] = np.ascontiguousarray(np.broadcast_to(inputs["b_spatial"][0].reshape(1, 1024), (128, 1024)))
    c["c_sgug"] = np.ascontiguousarray(np.broadcast_to(inputs["sgu_norm_g"][0][None, :], (128, 1024)))
    c["c_rb"] = np.ascontiguousarray(inputs["rel_bias"])
    c["c_rb31"] = np.ascontiguousarray(np.broadcast_to(inputs["rel_bias"][31][None, :], (32, 8)))
    oh = np.zeros((32, 2, 512), np.float32)
    cneg = np.zeros((128, 2, 256), np.float32)
    tq = np.arange(128)
    for q in range(2):
        p = (q + r) % 2
        u = np.arange(511)
        d = u - 127 + 128 * (p - 1)
        ok = d >= 0
        b = _bucket(np.maximum(d, 0))
        oh[b[ok], q, u[ok]] = 1.0
        mine = 2 * 0 + p
        for jb in range(2):
            if jb < p:
                m = np.zeros((128, 128), bool)
            elif jb == p:
                m = tq[None, :] > tq[:, None]
            else:
                m = np.ones((128, 128), bool)
            cneg[:, q, jb * 128:(jb + 1) * 128] = np.where(m, np.float32(-1e30), np.float32(0))
    c["c_oh"] = oh
    c["c_ones32"] = np.ones((32, 128), np.float32)
    c["c_cneg"] = cneg
    return c


OWN = {0: [0, 3, 4, 7, 8, 11, 12, 15], 1: [1, 2, 5, 6, 9, 10, 13, 14]}
_cache = {}


def make_in_maps(inputs, cores):
    maps = []
    x = np.asarray(inputs["x"], np.float32)
    shared = {
        "w_in": np.ascontiguousarray(inputs["w_in"][0]), "w_pa": np.ascontiguousarray(inputs["w_proj_a"][0]),
        "w_pb": np.ascontiguousarray(inputs["w_proj_b"][0]), "w_out": np.ascontiguousarray(inputs["w_out"][0]),
        "w_gate": np.ascontiguousarray(inputs["w_ffn_gate"][0]), "w_up": np.ascontiguousarray(inputs["w_ffn_up"][0]),
        "w_down": np.ascontiguousarray(inputs["w_ffn_down"][0]),
    }
    for cidx in cores:
        b, r = cidx // 2, cidx % 2
        m = dict(shared)
        m["x_all"] = np.ascontiguousarray(x[b])
        m["x_own"] = np.ascontiguousarray(x[b].reshape(16, 128, 2048)[OWN[r]].reshape(1024, 2048))
        m.update(host_consts(r, inputs))
        maps.append(m)
    return maps


def kernel(**inputs):
    inputs = {k: np.asarray(v) for k, v in inputs.items()}
    if "nc" not in _cache:
        _cache["nc"] = build()
    nc = _cache["nc"]
    cores = list(range(8))
    maps = make_in_maps(inputs, cores)
    res = run_bass_kernel_spmd(nc, maps, core_ids=cores)
    out = np.zeros((4, 16, 128, 2048), np.float32)
    for cidx in cores:
        b, r = cidx // 2, cidx % 2
        y = np.asarray(res.results[cidx]["y_own"]).reshape(8, 128, 2048)
        out[b, OWN[r]] = y
    return out.reshape(4, 2048, 2048)
```

```python
import math
from contextlib import ExitStack
import numpy as np
import concourse.bass as bass
import concourse.mybir as mybir
from concourse.bass_utils import run_bass_kernel_spmd

F32 = mybir.dt.float32
BF16 = mybir.dt.bfloat16
AF = mybir.ActivationFunctionType
ALU = mybir.AluOpType
AX = mybir.AxisListType

D = 2048
S = 2048
NH = 8
DFF = 5632
EPS = 1e-6
NIT = 16
ENG = ['pe', 'act', 'dve', 'pool', 'sp']
KB = 1024


class Buf:
    __slots__ = ('w', 'r', 'name')

    def __init__(self, name=''):
        self.w = []
        self.r = {}
        self.name = name


class Prog:
    def __init__(self):
        self.q = {e: [] for e in ENG}
        self.cnt = {e: 0 for e in ENG}
        self.waited = {e: {} for e in ENG}
        self.dma_n = {}
        self.dma_last = {}
        self.NDS = 8

    def _deps(self, eng, reads, writes):
        deps = {}

        def add(h):
            if h is None:
                return
            k, v = h
            if k == 'pe' and eng == 'pe':
                return
            if deps.get(k, 0) < v:
                deps[k] = v
        for b in reads:
            for h in b.w:
                add(h)
        for b in writes:
            for h in b.w:
                add(h)
            for k, v in b.r.items():
                add((k, v))
        waits = []
        for k, v in deps.items():
            if self.waited[eng].get(k, 0) < v:
                self.waited[eng][k] = v
                waits.append((k, v))
        return waits

    def _post(self, h, reads, writes, append=False):
        for b in writes:
            if append:
                b.w.append(h)
            else:
                b.w = [h]
                b.r = {}
        for b in reads:
            if b.r.get(h[0], 0) < h[1]:
                b.r[h[0]] = h[1]

    def op(self, eng, fn, reads=(), writes=()):
        waits = self._deps(eng, reads, writes)
        self.cnt[eng] += 1
        h = (eng, self.cnt[eng])
        self.q[eng].append((waits, fn, (eng, 1)))
        self._post(h, reads, writes)
        return h

    def dma(self, qeng, fn, reads=(), writes=(), append=False):
        n = self.dma_n.get(qeng, 0)
        self.dma_n[qeng] = n + 1
        key = 'd_%s_%d' % (qeng, n % self.NDS)
        val = 16 * (n // self.NDS + 1)
        waits = self._deps(qeng, reads, () if append else writes)
        if n >= self.NDS and self.waited[qeng].get(key, 0) < val - 16:
            self.waited[qeng][key] = val - 16
            waits.append((key, val - 16))
        h = (key, val)
        self.dma_last[key] = val
        self.q[qeng].append((waits, fn, (key, 16)))
        self._post(h, reads, writes, append)
        return h

    def barrier(self):
        tgt = {e: self.cnt[e] for e in ENG if self.cnt[e] > 0}
        tgt.update(self.dma_last)
        for e in ENG:
            waits = []
            for k, v in tgt.items():
                if k == e:
                    continue
                if self.waited[e].get(k, 0) < v:
                    self.waited[e][k] = v
                    waits.append((k, v))
            if waits:
                self.q[e].append((waits, None, None))


def build(ncores=8, stage=99, dbg=False):
    nc = bass.Bass("TRN2", target_bir_lowering=False)
    P = Prog()

    def din(name, shape, dt=F32):
        return nc.dram_tensor(name, list(shape), dt, kind="ExternalInput").ap()

    x_all = din("x_all", [2048, 2048])
    x_own = din("x_own", [1024, 2048])
    w_in = din("w_in", [2048, 10320])
    w_pa = din("w_pa", [1024, 2048])
    w_pb = din("w_pb", [1024, 2048])
    w_out = din("w_out", [2048, 2048])
    w_gate = din("w_gate", [2048, DFF])
    w_up = din("w_up", [2048, DFF])
    w_down = din("w_down", [DFF, 2048])
    c_bf = din("c_bf", [128, 384])
    c_g1b = din("c_g1b", [128, 2048])
    c_g2b = din("c_g2b", [128, 2048])
    c_sm = din("c_sm", [128, 16])
    c_tril = din("c_tril", [128, 128])
    c_wspT = din("c_wspT", [128, 8, 128])
    c_bsp = din("c_bsp", [128, 1024])
    c_sgug = din("c_sgug", [128, 1024])
    c_rb = din("c_rb", [32, 8])
    c_rb31 = din("c_rb31", [32, 8])
    c_oh = din("c_oh", [32, 2, 512])
    c_ones32 = din("c_ones32", [32, 128])
    c_cneg = din("c_cneg", [128, 2, 256])
    y_own = nc.dram_tensor("y_own", [1024, 2048], F32, kind="ExternalOutput").ap()
    Fscr_t = nc.dram_tensor("fscr", [16 * 128 * 512], F32, kind="Internal")
    Fscr = Fscr_t.ap()
    dbg_out = {}

    with ExitStack() as es:
        ARENA_KB = 204
        arena_t = es.enter_context(nc.sbuf_tensor("arena", [128, ARENA_KB * KB // 2], BF16))
        arena = arena_t[:]
        psb = []
        for i in range(8):
            t = es.enter_context(nc.psum_tensor("ps%d" % i, [128, 512], F32))
            psb.append(t[:])
        PS = [Buf('ps%d' % i) for i in range(8)]
        sems = {}
        for e in ENG:
            sems[e] = es.enter_context(nc.semaphore("s_" + e))
        for qe in ('sp', 'pool'):
            for i in range(P.NDS):
                k = 'd_%s_%d' % (qe, i)
                sems[k] = es.enter_context(nc.semaphore(k))

        def R(off, nbytes, dt=BF16, pat=None, **kw):
            a = arena[:, off // 2:(off + nbytes) // 2]
            if dt != BF16:
                a = a.bitcast(dt)
            if pat:
                a = a.rearrange(pat, **kw)
            return a

        def psbf(i):
            return psb[i].bitcast(BF16)

        ident = R(0, 256)
        ones = R(256, 256)
        onesbd = R(512, 256)
        csm = R(768, 64, F32)
        st = R(1024, 256, F32)
        B_st = Buf('st')
        B_c = Buf('consts')
        GB = R(4 * KB, 8 * KB, F32)
        B_gb = Buf('gb')
        RING = [R((12 + 8 * i) * KB, 8 * KB, BF16, "p (c n) -> p c n", c=16) for i in range(3)]
        B_ring = [Buf('ring%d' % i) for i in range(3)]
        out_aT = R(36 * KB, 16 * KB, BF16, "p (h t) -> p h t", h=8)
        B_oa = Buf('out_aT')
        kT = R(52 * KB, 32 * KB, BF16, "p (h t) -> p h t", h=8)
        B_kT = Buf('kT')
        V = R(84 * KB, 32 * KB, BF16, "p (j d) -> p j d", j=16)
        B_V = Buf('V')
        kiT2 = R(116 * KB, 4 * KB)
        B_ki = Buf('kiT2')
        wtok = R(120 * KB, 512, F32, "p (t h) -> p t h", t=8)
        B_wt = Buf('wtok')
        qT = R(124 * KB, 16 * KB, BF16, "p (h t) -> p h t", h=8)
        B_qT = Buf('qT')
        qiT = R(140 * KB, 16 * KB, BF16, "p (h t) -> p h t", h=8)
        B_qi = Buf('qiT')
        hT = R(156 * KB, 32 * KB, BF16, "p (c t) -> p c t", c=16)
        B_hT = [Buf('hT0'), Buf('hT1')]

        def pick(B, idx):
            return B[idx] if isinstance(B, list) else B
        xbuf = R(188 * KB, 8 * KB, F32)
        B_xb = Buf('xbuf')
        xn = R(196 * KB, 4 * KB)
        B_xn = Buf('xn')
        sqb = R(200 * KB, 1 * KB)
        B_sq = Buf('sqb')
        rstd = R(201 * KB, 2 * KB, F32)
        B_rs = Buf('rstd')

        def wait_all(e, waits):
            for k, v in waits:
                e.wait_ge(sems[k], v)

        def dma_sp(out, in_, reads=(), writes=()):
            P.dma('sp', lambda e, o=out, i=in_: e.dma_start(out=o, in_=i), reads, writes)

        def dma_pool(out, in_, reads=(), writes=()):
            P.dma('pool', lambda e, o=out, i=in_: e.dma_start(out=o, in_=i), reads, writes)

        ring_seq = [0]

        def load_block(specs):
            si = ring_seq[0] % 3
            ring_seq[0] += 1
            slot, b = RING[si], B_ring[si]
            first = True
            for (src, c0, col0) in specs:
                rows, cols = src.shape
                nchunk = rows // 128
                o = slot[:, c0:c0 + nchunk, col0:col0 + cols]
                i = src.rearrange("(c p) n -> p c n", p=128)
                P.dma('pool', lambda e, o=o, i=i: e.dma_start(out=o, in_=i), (), (b,), append=not first)
                first = False
            return slot, b

        def pe_op(fn, reads, writes, ring_bufs=()):
            return P.op('pe', fn, tuple(reads) + tuple(ring_bufs), writes)

        def mm_acc(e, out, pairs):
            ins = None
            n = len(pairs)
            for i, (l, r) in enumerate(pairs):
                ins = e.matmul(out, l, r, start=(i == 0), stop=(i == n - 1))
            return ins

        tasks = []
        pending = []

        def flush_pending():
            while pending:
                pending.pop(0)()

        BIGR = [R(12 * KB, 12 * KB), R(24 * KB, 12 * KB)]
        B_big = [Buf('big0'), Buf('big1')]
        big_seq = [0]

        def load_big(arg):
            nch, specs = arg if isinstance(arg, tuple) else (16, arg)
            si = big_seq[0] % 2
            big_seq[0] += 1
            view = BIGR[si][:, 0:nch * 256].rearrange("p (c n) -> p c n", c=nch)
            b = B_big[si]
            first = True
            for (src, c0, col0) in specs:
                rows, cols = src.shape
                nchunk = rows // 128
                o = view[:, c0:c0 + nchunk, col0:col0 + cols]
                i = src.rearrange("(c p) n -> p c n", p=128)
                P.dma('pool', lambda e, o=o, i=i: e.dma_start(out=o, in_=i), (), (b,), append=not first)
                first = False
            return view, b

        def run_tasks(big=False):
            loaders = [i for i, t in enumerate(tasks) if t[0] is not None]
            loaded = {}
            li = 0
            la = 1 if big else 2
            lfn = load_big if big else load_block
            for i, (ld, comp) in enumerate(tasks):
                if ld is not None:
                    my = loaders.index(i)
                    while li < len(loaders) and li <= my + la:
                        loaded[loaders[li]] = lfn(tasks[loaders[li]][0]())
                        li += 1
                    if not getattr(comp, '_fm', False):
                        flush_pending()
                    comp(*loaded.pop(i))
                else:
                    flush_pending()
                    comp()
            flush_pending()
            tasks.clear()

        dma_pool(R(0, 768), c_bf, (), (B_c,))
        dma_sp(csm, c_sm, (), (B_c,))
        dma_sp(GB, c_g1b, (), (B_gb,))

        psrot = [0]

        def next_ps(lo=0, n=3):
            i = lo + psrot[0] % n
            psrot[0] += 1
            return i

        xsets = {
            'a': [(xbuf, xn, B_xb, B_xn), (R(36 * KB, 8 * KB, F32), R(44 * KB, 4 * KB), Buf(), Buf()),
                  (R(124 * KB, 8 * KB, F32), R(132 * KB, 4 * KB), Buf(), Buf()),
                  (R(136 * KB, 8 * KB, F32), R(144 * KB, 4 * KB), Buf(), Buf())],
            'b': [(xbuf, xn, B_xb, B_xn), (R(166 * KB, 8 * KB, F32), R(174 * KB, 4 * KB), Buf(), Buf()),
                  (R(100 * KB, 8 * KB, F32), R(108 * KB, 4 * KB), Buf(), Buf()),
                  (R(112 * KB, 8 * KB, F32), R(120 * KB, 4 * KB), Buf(), Buf())],
        }
        B_sth = [Buf(), Buf(), Buf(), Buf()]
        hcnt = [0]

        def make_hT(src_rows, ntiles, dst, B_dst, tok0=0, xs='a', resident=None, xnbufs=None, gcols=None):
            tiles = []

            def s1(tt):
                par = hcnt[0] % 4
                hcnt[0] += 1
                sc = st[:, 3 * par:3 * par + 3]
                Bs = B_sth[par]
                if resident is None:
                    xb_, xn_, Bxb, Bxn = xsets[xs][par]
                    dma_sp(xb_, src_rows(tt), (), (Bxb,))
                else:
                    xb_, Bxb = resident(tt)
                    xn_, Bxn = xnbufs[tt % len(xnbufs)]
                P.op('act', lambda e: e.activation(out=xn_, in_=xb_, func=AF.Square, accum_out=sc[:, 0:1]),
                     (Bxb,), (Bxn, Bs))
                P.op('act', lambda e: e.activation(out=sc[:, 1:2], in_=sc[:, 0:1], func=AF.Ln, scale=1.0 / D, bias=EPS),
                     (Bs,), (Bs,))
                P.op('act', lambda e: e.activation(out=sc[:, 2:3], in_=sc[:, 1:2], func=AF.Exp, scale=-0.5), (Bs,), (Bs,))
                P.op('dve', lambda e: e.scalar_tensor_tensor(out=xn_, in0=xb_, scalar=sc[:, 2:3], in1=GB,
                                                              op0=ALU.mult, op1=ALU.mult),
                     (Bxb, Bs, B_gb), (Bxn,))
                tiles.append((xn_, Bxn))

            def s2(tt):
                xn_, Bxn = tiles[tt]
                for half in range(2):
                    pi = 5 + half
                    pv = psbf(pi)

                    def tr(e, half=half, pv=pv):
                        ins = None
                        for c in range(8):
                            cc = half * 8 + c
                            ins = e.transpose(pv[:, c * 128:(c + 1) * 128], xn_[:, cc * 128:(cc + 1) * 128], ident)
                        return ins
                    P.op('pe', tr, (Bxn, B_c), (PS[pi],))
                    o = dst[:, half * 8:half * 8 + 8, tok0 + tt * 128: tok0 + (tt + 1) * 128]
                    i_ = pv.rearrange("p (c t) -> p c t", c=8)
                    if half == 0:
                        P.op('act', lambda e, o=o, i_=i_: e.activation(out=o, in_=i_, func=AF.Copy), (PS[pi],), (pick(B_dst, (tok0 + tt * 128) // 512),))
                    else:
                        P.op('dve', lambda e, o=o, i_=i_: e.tensor_copy(out=o, in_=i_), (PS[pi],), (pick(B_dst, (tok0 + tt * 128) // 512),))
            LA = 2 if resident is None else 1
            for i in range(ntiles + LA):
                if i < ntiles:
                    s1(i)
                if i >= LA:
                    s2(i - LA)

        nsets = [(sqb, rstd, B_sq, B_rs), (R(48 * KB, 1 * KB), R(49 * KB, 2 * KB, F32), Buf(), Buf())]
        ncnt = [0]

        def norm_post(pi, dst, B_dst, gcol, ones_ap, inv_n, npart=128):
            ps = psb[pi]
            sq_, rs_, Bsq, Brs = nsets[ncnt[0] % 2]
            p2 = 3 + (ncnt[0] % 2)
            ncnt[0] += 1
            P.op('act', lambda e: e.activation(out=sq_, in_=ps, func=AF.Square), (PS[pi],), (Bsq,))
            P.op('pe', lambda e: e.matmul(psb[p2], ones_ap, sq_, start=True, stop=True), (Bsq, B_c), (PS[p2],))
            P.op('act', lambda e: e.activation(out=rs_, in_=psb[p2], func=AF.Ln, scale=inv_n, bias=EPS),
                 (PS[p2],), (Brs,))
            P.op('act', lambda e: e.activation(out=rs_, in_=rs_, func=AF.Exp, scale=-0.5), (Brs,), (Brs,))
            P.op('dve', lambda e: e.scalar_tensor_tensor(out=dst, in0=ps, scalar=gcol, in1=rs_,
                                                          op0=ALU.mult, op1=ALU.mult),
                 (PS[pi], Brs, B_c), (B_dst,))

        def fm_block_task(wsrc, col0, consumer, src_hT=None, B_src=None, ntokchunks=2, tok0=0, ncc=2, K=16, M=128):
            src_hT = hT if src_hT is None else src_hT
            B_src = B_hT if B_src is None else B_src

            def ld():
                if K == 16:
                    return [(wsrc[0:1024, col0:col0 + ncc * M], 0, 0), (wsrc[1024:2048, col0:col0 + ncc * M], 8, 0)]
                return [(wsrc[0:1024, col0:col0 + ncc * M], 0, 0)]

            def comp(slot, b):
                for cc in range(ncc):
                    for n in range(ntokchunks):
                        pi = next_ps()
                        pairs = [(slot[:, c, cc * M:(cc + 1) * M], src_hT[:, c, tok0 + n * 512: tok0 + (n + 1) * 512])
                                 for c in range(K)]
                        pe_op(lambda e, pi=pi, pairs=pairs: mm_acc(e, psb[pi], pairs), (pick(B_src, (tok0 + n * 512) // 512),), (PS[pi],), (b,))
                        flush_pending()
                        pending.append(lambda cc=cc, n=n, pi=pi: consumer(cc, n, pi))
            comp._fm = True
            tasks.append((ld, comp))

        def tm_block_task(wsrc, col0, ncols, consumer, lhs, B_lhs, ntiles=8, K=16, row0=0):
            def ld():
                sp = []
                r = 0
                c0 = 0
                while r < K * 128:
                    nr = min(1024, K * 128 - r)
                    sp.append((wsrc[row0 + r:row0 + r + nr, col0:col0 + ncols], c0, 0))
                    r += nr
                    c0 += nr // 128
                return sp

            def comp(slot, b):
                flush_pending()
                for tt in range(ntiles):
                    pi = next_ps()
                    pairs = [(lhs(c, tt), slot[:, c, 0:ncols]) for c in range(K)]
                    pe_op(lambda e, pi=pi, pairs=pairs: mm_acc(e, psb[pi][:, 0:ncols], pairs), (pick(B_lhs, tt // 4),), (PS[pi],), (b,))
                    consumer(tt, pi)
            tasks.append((ld, comp))

        for half in range(2):
            tasks.append((None, lambda half=half: make_hT(
                lambda tt, half=half: x_all[half * 1024 + tt * 128: half * 1024 + (tt + 1) * 128, :], 8, hT, B_hT)))
            for blk in range(4):
                def cons_k(cc, n, pi, blk=blk, half=half):
                    head = 2 * blk + cc
                    t0 = half * 1024 + n * 512
                    norm_post(pi, kT[:, head, t0:t0 + 512], B_kT, csm[:, 1:2], ones, 1.0 / 128)
                fm_block_task(w_in, 1024 + 256 * blk, cons_k)
            for blk in range(4):
                def cons_v(tt, pi, blk=blk, half=half):
                    o = V[:, half * 8 + tt, 256 * blk:256 * blk + 256]
                    P.op('act', lambda e, o=o, pi=pi: e.activation(out=o, in_=psb[pi][:, 0:256], func=AF.Copy),
                         (PS[pi],), (B_V,))
                tm_block_task(w_in, 2048 + 256 * blk, 256, cons_v,
                              lambda c, tt: hT[:, c, tt * 128:(tt + 1) * 128], B_hT)

            def ld_ki():
                sp = []
                for j in range(2):
                    for dup in range(2):
                        sp.append((w_in[j * 1024:(j + 1) * 1024, 4096:4160], 8 * j, 64 * dup))
                return sp

            def comp_ki(slot, b, half=half):
                for n in range(2):
                    pi = next_ps()
                    pairs = [(slot[:, c, 0:128], hT[:, c, n * 512:(n + 1) * 512]) for c in range(16)]
                    pe_op(lambda e, pi=pi, pairs=pairs: mm_acc(e, psb[pi], pairs), (pick(B_hT, n),), (PS[pi],), (b,))
                    t0 = half * 1024 + n * 512
                    norm_post(pi, kiT2[:, t0:t0 + 512], B_ki, csm[:, 2:3], onesbd, 1.0 / 64)
            tasks.append((ld_ki, comp_ki))

        tasks.append((None, lambda: make_hT(lambda tt: x_own[tt * 128:(tt + 1) * 128, :], 8, hT, B_hT)))
        for blk in range(4):
            def cons_q(cc, n, pi, blk=blk):
                head = 2 * blk + cc
                norm_post(pi, qT[:, head, n * 512:(n + 1) * 512], B_qT, csm[:, 0:1], ones, 1.0 / 128)
            fm_block_task(w_in, 256 * blk, cons_q)
        for blk in range(4):
            def cons_qi(cc, n, pi, blk=blk):
                o = qiT[:, 2 * blk + cc, n * 512:(n + 1) * 512]
                P.op('act', lambda e, o=o, pi=pi: e.activation(out=o, in_=psb[pi], func=AF.Copy), (PS[pi],), (B_qi,))
            fm_block_task(w_in, 3072 + 256 * blk, cons_qi)

        def cons_wi(tt, pi):
            P.op('dve', lambda e, tt=tt, pi=pi: e.tensor_copy(out=wtok[:, tt, :], in_=psb[pi][:, 0:16]), (PS[pi],), (B_wt,))
        tm_block_task(w_in, 4160, 16, cons_wi, lambda c, tt: hT[:, c, tt * 128:(tt + 1) * 128], B_hT)

        run_tasks()
        P.barrier()

        if dbg and stage == 2:
            for nm, ap_, shp in (("kT", kT, [128, 8, 2048]), ("V", V, [128, 16, 1024]), ("qT", qT, [128, 8, 1024]),
                                 ("qiT", qiT, [128, 8, 1024])):
                d = nc.dram_tensor("dbg_" + nm, shp, BF16, kind="ExternalOutput").ap()
                dma_sp(d, ap_)
            d = nc.dram_tensor("dbg_kiT2", [128, 2048], BF16, kind="ExternalOutput").ap()
            dma_sp(d, kiT2)
            d = nc.dram_tensor("dbg_wtok", [128, 8, 16], F32, kind="ExternalOutput").ap()
            dma_sp(d, wtok)

        if stage >= 3:
            accs = [R(156 * KB, 8 * KB, F32), R(164 * KB, 8 * KB, F32)]
            m01s = [R(172 * KB, 4 * KB), R(176 * KB, 4 * KB)]
            maskT = R(180 * KB, 4 * KB, BF16, "p (j t) -> p j t", j=16)
            MW = R(184 * KB, 6 * KB, BF16, "p (h b t) -> p h b t", h=8, b=3)
            EBW = R(190 * KB, 12 * KB, BF16, "p (q h b t) -> p q h b t", q=2, h=8, b=3)
            rden = R(202 * KB, 512, F32)
            Dg = [R(12 * KB, 4 * KB, BF16, "p (h t) -> p h t", h=16), R(16 * KB, 4 * KB, BF16, "p (h t) -> p h t", h=16)]
            Rt = [R(20 * KB, 1 * KB), R(21 * KB, 1 * KB), R(22 * KB, 1 * KB)]
            PT = [R(23 * KB, 1 * KB), R(24 * KB, 1 * KB), R(203 * KB, 1 * KB)]
            cneg = R(25 * KB, 2 * KB, F32, "p (q s) -> p q s", q=2)
            sgnw = R(27 * KB, 512, F32, "p (t h) -> p t h", t=8)
            absw = R(27 * KB + 512, 512, F32, "p (t h) -> p t h", t=8)
            B_ebw, B_cn, B_mT, B_mw, B_rd, B_sw = (Buf() for _ in range(6))
            B_acc = [Buf(), Buf()]
            B_m01 = [Buf(), Buf()]
            B_dg = [Buf(), Buf()]
            B_rt = [Buf(), Buf(), Buf()]
            B_pt = [Buf(), Buf(), Buf()]
            B_sb = [Buf(), Buf()]
            B_den = [Buf(), Buf()]
            oh = R(28 * KB, 4 * KB, F32, "p (q u) -> p q u", q=2)[0:32]
            Lh = R(32 * KB, 4 * KB, F32, "p (h m) -> p h m", h=8)[0:32]
            rb = R(164 * KB, 32, F32)[0:32, :]
            rb31 = R(164 * KB + 32, 32, F32)[0:32, :]
            ones32 = R(164 * KB + 64, 512, F32)[0:32]
            vld = R(165 * KB, 2 * KB, F32)
            e1 = R(167 * KB, 2 * KB, F32)
            stage_t = R(169 * KB, 1536, F32)
            B_eb = Buf()
            B_e1 = Buf()
            B_stg = Buf()
            dma_sp(rb, c_rb, (), (B_eb,))
            dma_sp(rb31, c_rb31, (), (B_eb,))
            dma_sp(oh, c_oh, (), (B_eb,))
            dma_sp(ones32, c_ones32, (), (B_eb,))
            dma_sp(cneg, c_cneg, (), (B_cn,))
            P.op('dve', lambda e: e.tensor_tensor(out=rb, in0=rb, in1=rb31, op=ALU.subtract), (B_eb,), (B_eb,))
            for h in range(8):
                P.op('dve', lambda e, h=h: e.tensor_scalar(out=Lh[:, h, :], in0=ones32, scalar1=rb[:, h:h + 1], scalar2=None,
                                                            op0=ALU.mult), (B_eb,), (B_eb,))
            P.op('act', lambda e: e.activation(out=sgnw, in_=wtok, func=AF.Sign), (B_wt,), (B_sw,))
            P.op('dve', lambda e: e.tensor_tensor(out=absw, in0=wtok, in1=sgnw, op=ALU.mult), (B_wt, B_sw), (B_sw,))
            e1s = [R((156 + 2 * i) * KB, 2 * KB, F32) for i in range(8)]
            stgs = [R(172 * KB + 1536 * i, 1536, F32) for i in range(8)]
            B_e1s = [Buf() for _ in range(8)]
            B_stgs = [Buf() for _ in range(8)]
            vlds = [R(185 * KB, 2 * KB, F32), R(187 * KB, 2 * KB, F32)]
            B_vld = [Buf(), Buf()]
            for q in range(2):
                P.op('pe', lambda e, q=q: e.matmul(psb[0], ones32, oh[:, q, :], start=True, stop=True), (B_eb,), (PS[0],))
                P.op('dve', lambda e, q=q: e.tensor_copy(out=vlds[q], in_=psb[0]), (PS[0],), (B_vld[q],))
            for q in range(2):
                for h in range(8):
                    i8 = h
                    pi = 1 + (h % 3)
                    e1 = e1s[i8]
                    P.op('pe', lambda e, q=q, h=h, pi=pi: e.matmul(psb[pi], Lh[:, h, :], oh[:, q, :], start=True, stop=True),
                         (B_eb,), (PS[pi],))
                    P.op('act', lambda e, pi=pi, e1=e1: e.activation(out=e1, in_=psb[pi], func=AF.Exp), (PS[pi],), (B_e1s[i8],))
                    P.op('dve', lambda e, e1=e1, q=q: e.tensor_tensor(out=e1, in0=e1, in1=vlds[q], op=ALU.mult),
                         (B_e1s[i8], B_vld[q]), (B_e1s[i8],))
                    base = (h * 2 + q) * 65536
                    dma_sp(Fscr[base:base + 65536].rearrange("(r u) -> r u", u=512), e1, (B_e1s[i8],), (B_stgs[i8],))
                for h in range(8):
                    i8 = h
                    base = (h * 2 + q) * 65536
                    src = Fscr[base + 127: base + 127 + 128 * 511].rearrange("(s c) -> s c", c=511)[:, 0:384]
                    dma_pool(stgs[i8], src, (B_stgs[i8],), (B_stgs[i8],))
                    for b3 in range(3):
                        P.op('dve', lambda e, q=q, h=h, b3=b3, i8=i8: e.tensor_copy(
                            out=EBW[:, q, h, b3, :], in_=stgs[i8][:, (2 - b3) * 128:(3 - b3) * 128]), (B_stgs[i8],), (B_ebw,))
            P.barrier()

            SCALE = 128.0 ** -0.5
            SB0 = 24

            def gen_A(k):
                th = []
                kp = k % 2
                nk = 2 * k + 2
                Sk = 128 * nk
                items = [(c0, min(512, Sk - c0), h) for c0 in range(0, Sk, 512) for h in range(16)]

                def x_op(i):
                    c0, wc, h = items[i]
                    pi = i % 2
                    pr = (h % 2) * 64
                    P.op('pe', lambda e: e.matmul(
                        psb[pi][:, 0:wc], qiT[pr:pr + 64, h // 2, k * 128:(k + 1) * 128], kiT2[pr:pr + 64, c0:c0 + wc],
                        start=True, stop=True), (B_qi, B_ki), (PS[pi],))

                def mid_op(i):
                    c0, wc, h = items[i]
                    pi = i % 2
                    ri = i % 3
                    P.op('act', lambda e: e.activation(out=Rt[ri][:, 0:wc], in_=psb[pi][:, 0:wc], func=AF.Relu,
                                                       scale=absw[:, k, h:h + 1]), (PS[pi], B_sw), (B_rt[ri],))

                def fin_op(i):
                    c0, wc, h = items[i]
                    ri = i % 3
                    P.op('pe', lambda e: e.matmul(psb[7][:, 0:wc], Dg[kp][:, h, :], Rt[ri][:, 0:wc],
                                                  start=(h == 0), stop=(h == 15)), (B_dg[kp], B_rt[ri]), (PS[7],))
                    if h == 15:
                        P.op('dve', lambda e: e.tensor_copy(out=accs[kp][:, c0:c0 + wc], in_=psb[7][:, 0:wc]),
                             (PS[7],), (B_acc[kp],))
                for i in range(len(items) + 1):
                    def unit_a(i=i):
                        if i < len(items):
                            x_op(i)
                        if i >= 1:
                            mid_op(i - 1)

                    def unit_b(i=i):
                        if i >= 1:
                            fin_op(i - 1)
                    th.append(unit_a)
                    th.append(unit_b)
                return th

            def gen_D(k):
                kp = k % 2
                return [lambda: P.op('dve', lambda e: e.tensor_tensor(
                    out=Dg[kp], in0=ident.unsqueeze(1).broadcast_to([128, 16, 128]),
                    in1=sgnw[:, k, :].unsqueeze(2).broadcast_to([128, 16, 128]), op=ALU.mult), (B_c, B_sw), (B_dg[kp],))]

            def gen_B(k):
                th = []
                kp = k % 2
                q = k % 2
                nk = 2 * k + 2
                Sk = 128 * nk
                acc = accs[kp]
                m01 = m01s[kp]
                s0 = SB0 + 8 * kp
                cB, cW, cLo, cMid, cCnt, cG = (st[:, s0 + i:s0 + i + 1] for i in range(6))
                Bs = B_sb[kp]
                th.append(lambda: P.op('dve', lambda e: e.tensor_reduce(out=cB, in_=acc[:, 0:Sk], axis=AX.X, op=ALU.max,
                                                                       apply_absolute_value=True), (B_acc[kp],), (Bs,)))
                th.append(lambda: P.op('dve', lambda e: e.tensor_tensor(out=acc[:, Sk - 256:Sk], in0=acc[:, Sk - 256:Sk],
                                                                       in1=cneg[:, q, :], op=ALU.add), (B_acc[kp], B_cn), (B_acc[kp],)))
                th.append(lambda: P.op('dve', lambda e: e.tensor_scalar(out=cW, in0=cB, scalar1=2.0, scalar2=2.0,
                                                                       op0=ALU.mult, op1=ALU.add), (Bs,), (Bs,)))
                th.append(lambda: P.op('dve', lambda e: e.tensor_scalar(out=cLo, in0=cB, scalar1=-1.0, scalar2=-1.0,
                                                                       op0=ALU.mult, op1=ALU.add), (Bs,), (Bs,)))
                for it in range(NIT):
                    f = 2.0 ** -(it + 1)
                    th.append(lambda f=f: P.op('dve', lambda e: e.scalar_tensor_tensor(out=cMid, in0=cW, scalar=f, in1=cLo,
                                                                                       op0=ALU.mult, op1=ALU.add), (Bs,), (Bs,)))
                    th.append(lambda: P.op('dve', lambda e: e.tensor_scalar(out=m01[:, 0:Sk], in0=acc[:, 0:Sk], scalar1=cMid,
                                                                           scalar2=None, op0=ALU.is_ge, op1=ALU.add, accum_out=cCnt),
                                           (B_acc[kp], Bs), (B_m01[kp], Bs)))
                    th.append(lambda f=f: P.op('dve', lambda e: e.tensor_scalar(out=cG, in0=cCnt, scalar1=255.5, scalar2=f,
                                                                               op0=ALU.is_ge, op1=ALU.mult), (Bs,), (Bs,)))
                    th.append(lambda: P.op('dve', lambda e: e.scalar_tensor_tensor(out=cLo, in0=cG, scalar=cW, in1=cLo,
                                                                                   op0=ALU.mult, op1=ALU.add), (Bs,), (Bs,)))
                th.append(lambda: P.op('dve', lambda e: e.tensor_scalar(out=m01[:, 0:Sk], in0=acc[:, 0:Sk], scalar1=cLo,
                                                                       scalar2=None, op0=ALU.is_ge), (B_acc[kp], Bs), (B_m01[kp],)))
                return th

            def gen_C(k):
                th = []
                kp = k % 2
                q = k % 2
                nk = 2 * k + 2
                m01 = m01s[kp]
                for j0 in range(0, nk, 8):
                    j1 = min(nk, j0 + 8)

                    def trunit(j0=j0, j1=j1):
                        pv = psbf(6)

                        def trm(e):
                            ins = None
                            for j in range(j0, j1):
                                ins = e.transpose(pv[:, (j - j0) * 128:(j - j0 + 1) * 128], m01[:, j * 128:(j + 1) * 128], ident)
                            return ins
                        P.op('pe', trm, (B_m01[kp], B_c), (PS[6], B_den[0], B_den[1]))
                        P.op('act', lambda e: e.activation(
                            out=maskT[:, j0:j1, :], in_=pv[:, 0:(j1 - j0) * 128].rearrange("p (j t) -> p j t", t=128),
                            func=AF.Copy), (PS[6], B_den[0], B_den[1]), (B_mT,))
                    th.append(trunit)
                b_lo = 1 if k == 0 else 0
                jw0 = 2 * k - 1 + b_lo
                nb3 = 3 - b_lo
                th.append(lambda: P.op('dve', lambda e: e.tensor_tensor(
                    out=MW[:, :, b_lo:3, :], in0=maskT[:, jw0:nk, :].unsqueeze(1).broadcast_to([128, 8, nb3, 128]),
                    in1=EBW[:, q, :, b_lo:3, :], op=ALU.mult), (B_mT, B_ebw), (B_mw,)))
                npro = len(th)
                items = [(h, g0) for h in range(8) for g0 in range(0, nk, 4)]

                def qk_op(i):
                    h, g0 = items[i]
                    g1 = min(nk, g0 + 4)
                    pl = 2 + (i % 2)

                    def qk(e):
                        ins = None
                        for j in range(g0, g1):
                            ins = e.matmul(psb[pl][:, (j - g0) * 128:(j - g0 + 1) * 128], kT[:, h, j * 128:(j + 1) * 128],
                                           qT[:, h, k * 128:(k + 1) * 128], start=True, stop=True)
                        return ins
                    P.op('pe', qk, (B_kT, B_qT), (PS[pl],))

                def mid_op(i):
                    h, g0 = items[i]
                    g1 = min(nk, g0 + 4)
                    wg = (g1 - g0) * 128
                    pl = 2 + (i % 2)
                    pti = i % 3
                    P.op('act', lambda e: e.activation(
                        out=PT[pti][:, 0:wg], in_=psb[pl][:, 0:wg], func=AF.Exp, scale=SCALE, bias=csm[:, 4 + h:5 + h]),
                        (PS[pl], B_c), (B_pt[pti],))
                    fa1 = min(g1, jw0)
                    if fa1 > g0:
                        wf = (fa1 - g0) * 128
                        P.op('dve', lambda e: e.tensor_tensor(
                            out=PT[pti][:, 0:wf].rearrange("p (j t) -> p j t", t=128),
                            in0=PT[pti][:, 0:wf].rearrange("p (j t) -> p j t", t=128),
                            in1=maskT[:, g0:fa1, :], op=ALU.mult), (B_pt[pti], B_mT), (B_pt[pti],))
                    wa0 = max(g0, jw0)
                    if g1 > wa0:
                        o0 = (wa0 - g0) * 128
                        bb0 = wa0 - (2 * k - 1)
                        bb1 = g1 - (2 * k - 1)
                        P.op('dve', lambda e: e.tensor_tensor(
                            out=PT[pti][:, o0:wg].rearrange("p (j t) -> p j t", t=128),
                            in0=PT[pti][:, o0:wg].rearrange("p (j t) -> p j t", t=128),
                            in1=MW[:, h, bb0:bb1, :], op=ALU.mult), (B_pt[pti], B_mw), (B_pt[pti],))

                def fin_op(i):
                    h, g0 = items[i]
                    g1 = min(nk, g0 + 4)
                    pti = i % 3
                    po = 4 + (h % 2)
                    hh = h % 2

                    def pv_(e):
                        ins = None
                        for j in range(g0, g1):
                            pj = PT[pti][:, (j - g0) * 128:(j - g0 + 1) * 128]
                            e.matmul(psb[po][:, 0:128], V[:, j, h * 128:(h + 1) * 128], pj, start=(j == 0), stop=(j == nk - 1))
                            ins = e.matmul(psb[6][:, hh * 128:hh * 128 + 128], ones, pj, start=(j == 0), stop=(j == nk - 1))
                        return ins
                    P.op('pe', pv_, (B_V, B_pt[pti], B_c), (PS[po], B_den[hh]))
                    if g1 == nk:
                        P.op('act', lambda e: e.activation(out=rden, in_=psb[6][:, hh * 128:hh * 128 + 128], func=AF.Ln), (B_den[hh],), (B_rd,))
                        P.op('act', lambda e: e.activation(out=rden, in_=rden, func=AF.Exp, scale=-1.0), (B_rd,), (B_rd,))
                        P.op('dve', lambda e: e.tensor_tensor(
                            out=out_aT[:, h, k * 128:(k + 1) * 128], in0=psb[po][:, 0:128], in1=rden, op=ALU.mult),
                            (PS[po], B_rd), (B_oa,))
                for i in range(len(items) + 1):
                    def unit_a(i=i):
                        if i < len(items):
                            qk_op(i)
                        if i >= 1:
                            mid_op(i - 1)

                    def unit_b(i=i):
                        if i >= 1:
                            fin_op(i - 1)
                    th.append(unit_a)
                    th.append(unit_b)
                return th[:npro], th[npro:]

            def merge_run(lists):
                lists = [l for l in lists if l]
                pos = [0] * len(lists)
                while True:
                    best = -1
                    bf = 2.0
                    for i, l in enumerate(lists):
                        if pos[i] < len(l):
                            fr = pos[i] / len(l)
                            if fr < bf:
                                bf = fr
                                best = i
                    if best < 0:
                        break
                    lists[best][pos[best]]()
                    pos[best] += 1

            for r_ in range(-3, 8):
                ls = []
                if 0 <= r_ < 8:
                    pro, main = gen_C(r_)
                    for t_ in pro:
                        t_()
                    ls.append(main)
                if 0 <= r_ + 3 < 8:
                    ls.append(gen_D(r_ + 3))
                if 0 <= r_ + 2 < 8:
                    ls.append(gen_A(r_ + 2))
                if 0 <= r_ + 1 < 8:
                    ls.append(gen_B(r_ + 1))
                merge_run(ls)
            P.barrier()

        if dbg and stage == 3:
            d = nc.dram_tensor("dbg_oaT", [128, 8, 1024], BF16, kind="ExternalOutput").ap()
            dma_sp(d, out_aT)

        if stage >= 4:
            hT = R(52 * KB, 32 * KB, BF16, "p (c t) -> p c t", c=16)
            uT = R(84 * KB, 16 * KB, BF16, "p (g t) -> p g t", g=8)
            gv = R(100 * KB, 32 * KB, F32, "p (t d) -> p t d", t=8)
            vn = R(132 * KB, 16 * KB, BF16, "p (t d) -> p t d", t=8)
            wspT = R(148 * KB, 2 * KB, BF16, "p (g t) -> p g t", g=8)
            bspb = R(150 * KB, 4 * KB, F32)
            sgug = R(154 * KB, 4 * KB, F32)
            wspf = R(158 * KB, 4 * KB, F32, "p (g t) -> p g t", g=8)
            tril = R(162 * KB, 512, F32)
            ztmp = R(163 * KB, 2 * KB, F32)
            sa = R(166 * KB, 8 * KB, F32, "p (a n t) -> p a n t", a=2, n=2)
            sb = R(174 * KB, 8 * KB, F32, "p (a n t) -> p a n t", a=2, n=2)
            m1 = R(182 * KB, 6 * KB, F32)
            m1t = [R(182 * KB, 2 * KB, F32), R(184 * KB, 2 * KB, F32), R(186 * KB, 2 * KB, F32), R(201 * KB, 2 * KB, F32)]
            mergedT = R(100 * KB, 32 * KB, BF16, "p (c t) -> p c t", c=16)
            B_hT = [Buf(), Buf()]
            B_uT, B_gv, B_vn, B_wsp, B_zt, B_sa, B_sb, B_m1, B_mg = (Buf() for _ in range(9))
            B_mg = B_gv
            x1 = R(132 * KB, 64 * KB, F32, "p (t d) -> p t d", t=8)
            B_x1 = [Buf() for _ in range(8)]

            def ld_x1_early():
                for tt, extra in ((0, (B_vn,)), (1, (B_vn,)), (2, (B_wsp,)), (3, (B_wsp, B_zt))):
                    dma_sp(x1[:, tt, :], x_own[tt * 128:(tt + 1) * 128, :], (), (B_x1[tt],) + extra)
            dma_sp(wspf, c_wspT, (), (B_wsp,))
            dma_sp(tril, c_tril, (), (B_wsp,))
            dma_sp(bspb, c_bsp, (), (B_wsp,))
            dma_sp(sgug, c_sgug, (), (B_wsp,))
            for g in range(8):
                P.op('dve', lambda e, g=g: e.tensor_tensor(out=wspT[:, g, :], in0=wspf[:, g, :], in1=tril, op=ALU.mult),
                     (B_wsp,), (B_wsp,))
            tasks.append((None, lambda: make_hT(lambda tt: x_own[tt * 128:(tt + 1) * 128, :], 8, hT, B_hT, xs='b')))
            for blk in range(4):
                def cons_u(cc, n, pi, blk=blk):
                    o = uT[:, 2 * blk + cc, n * 512:(n + 1) * 512]
                    P.op('act', lambda e, o=o, pi=pi: e.activation(out=o, in_=psb[pi], func=AF.Gelu_apprx_tanh),
                         (PS[pi],), (B_uT,))
                fm_block_task(w_in, 4176 + 256 * blk, cons_u, src_hT=hT, B_src=B_hT)
            for blk in range(4):
                def cons_vb(tt, pi, blk=blk):
                    o = gv[:, tt, 256 * blk:256 * blk + 256]
                    P.op('act', lambda e, o=o, pi=pi: e.activation(out=o, in_=psb[pi][:, 0:256], func=AF.Gelu_apprx_tanh),
                         (PS[pi],), (B_gv,))
                tm_block_task(w_in, 5200 + 256 * blk, 256, cons_vb, lambda c, tt: hT[:, c, tt * 128:(tt + 1) * 128], B_hT)

            def sgu_core():
                for tt in range(8):
                    P.op('act', lambda e, tt=tt: e.activation(out=vn[:, tt, :], in_=gv[:, tt, :], func=AF.Square,
                                                              accum_out=st[:, 16:17]), (B_gv,), (B_vn, B_st))
                    P.op('act', lambda e: e.activation(out=st[:, 17:18], in_=st[:, 16:17], func=AF.Ln, scale=1.0 / 1024,
                                                       bias=EPS), (B_st,), (B_st,))
                    P.op('act', lambda e: e.activation(out=st[:, 18:19], in_=st[:, 17:18], func=AF.Exp, scale=-0.5), (B_st,), (B_st,))
                    P.op('dve', lambda e, tt=tt: e.scalar_tensor_tensor(out=vn[:, tt, :], in0=gv[:, tt, :], scalar=st[:, 18:19],
                                                                         in1=sgug, op0=ALU.mult, op1=ALU.mult),
                         (B_gv, B_st, B_wsp), (B_vn,))
                    for gh in range(2):
                        pi = 5 + gh

                        def zmm(e, tt=tt, gh=gh, pi=pi):
                            ins = None
                            for g4 in range(4):
                                g = gh * 4 + g4
                                ins = e.matmul(psb[pi][:, g4 * 128:(g4 + 1) * 128], vn[:, tt, g * 128:(g + 1) * 128],
                                               wspT[:, g, :], start=True, stop=True)
                            return ins
                        P.op('pe', zmm, (B_vn, B_wsp), (PS[pi],))
                        P.op('dve', lambda e, gh=gh, pi=pi: e.tensor_tensor(out=ztmp, in0=psb[pi], in1=bspb[:, gh * 512:(gh + 1) * 512],
                                                                            op=ALU.add), (PS[pi], B_wsp), (B_zt,))
                        P.op('dve', lambda e, gh=gh, tt=tt: e.tensor_tensor(
                            out=uT[:, gh * 4:gh * 4 + 4, tt * 128:(tt + 1) * 128],
                            in0=ztmp.rearrange("p (g t) -> p g t", g=4),
                            in1=uT[:, gh * 4:gh * 4 + 4, tt * 128:(tt + 1) * 128], op=ALU.mult), (B_zt, B_uT), (B_uT,))
            for c2 in range(8):
                def cons_ga(cc, n, pi):
                    P.op('act', lambda e, cc=cc, n=n, pi=pi: e.activation(out=sa[:, cc, n, :], in_=psb[pi], func=AF.Sigmoid),
                         (PS[pi],), (B_sa,))
                fm_block_task(w_in, 6224 + 256 * c2, cons_ga, src_hT=hT, B_src=B_hT)

                def cons_gb(cc, n, pi):
                    P.op('act', lambda e, cc=cc, n=n, pi=pi: e.activation(out=sb[:, cc, n, :], in_=psb[pi], func=AF.Sigmoid),
                         (PS[pi],), (B_sb,))
                fm_block_task(w_in, 8272 + 256 * c2, cons_gb, src_hT=hT, B_src=B_hT)
                if c2 == 0:
                    tasks.append((None, sgu_core))
                    if stage >= 5:
                        tasks.append((None, ld_x1_early))

                def cons_pa(cc, n, pi):
                    P.op('dve', lambda e, cc=cc, n=n, pi=pi: e.tensor_tensor(out=m1t[cc * 2 + n], in0=psb[pi], in1=sa[:, cc, n, :],
                                                                             op=ALU.mult), (PS[pi], B_sa), (B_m1,))
                fm_block_task(w_pa, 256 * c2, cons_pa, src_hT=out_aT, B_src=B_oa, K=8)

                def cons_pb(cc, n, pi, c2=c2):
                    P.op('dve', lambda e, cc=cc, n=n, pi=pi: e.tensor_tensor(out=sb[:, cc, n, :], in0=psb[pi], in1=sb[:, cc, n, :],
                                                                             op=ALU.mult), (PS[pi], B_sb), (B_sb,))
                    P.op('dve', lambda e, cc=cc, n=n, c2=c2: e.tensor_tensor(
                        out=mergedT[:, 2 * c2 + cc, n * 512:(n + 1) * 512], in0=sb[:, cc, n, :], in1=m1t[cc * 2 + n], op=ALU.add),
                        (B_sb, B_m1), (B_mg,))
                fm_block_task(w_pb, 256 * c2, cons_pb, src_hT=uT, B_src=B_uT, K=8)
            if stage < 5:
                run_tasks()
                P.barrier()
            else:
                tasks.append((None, P.barrier))

        if dbg and stage == 4:
            d = nc.dram_tensor("dbg_mg", [128, 16, 1024], BF16, kind="ExternalOutput").ap()
            dma_sp(d, mergedT)
            d = nc.dram_tensor("dbg_obT", [128, 8, 1024], BF16, kind="ExternalOutput").ap()
            dma_sp(d, uT)

        if stage >= 5:
            def ld_x1_late():
                for tt in range(4, 8):
                    dma_sp(x1[:, tt, :], x_own[tt * 128:(tt + 1) * 128, :], (), (B_x1[tt],))
                if stage >= 6:
                    dma_sp(GB, c_g2b, (), (B_gb,))
            tasks.append((None, ld_x1_late))
            for blk in range(8):
                def cons_o(tt, pi, blk=blk):
                    o = x1[:, tt, 256 * blk:256 * blk + 256]
                    P.op('dve', lambda e, o=o, pi=pi: e.tensor_tensor(out=o, in0=o, in1=psb[pi][:, 0:256], op=ALU.add),
                         (PS[pi], B_x1[tt]), (B_x1[tt],))
                tm_block_task(w_out, 256 * blk, 256, cons_o, lambda c, tt: mergedT[:, c, tt * 128:(tt + 1) * 128], B_mg)
            if stage < 6:
                run_tasks()
                P.barrier()

        if stage >= 6:
            h2T = R(36 * KB, 32 * KB, BF16, "p (c t) -> p c t", c=16)
            aT = R(68 * KB, 44 * KB, BF16, "p (f t) -> p f t", f=22)
            xn2 = R(196 * KB, 4 * KB)
            xn2b = R(200 * KB, 4 * KB)
            sg = [R((112 + 2 * i) * KB, 2 * KB, F32) for i in range(4)]
            B_h2 = [Buf(), Buf()]
            B_aT, B_xn2, B_xn2b = Buf(), Buf(), Buf()
            B_sg = [Buf() for _ in range(4)]
            tasks.append((None, lambda: make_hT(None, 8, h2T, B_h2, resident=lambda tt: (x1[:, tt, :], B_x1[tt]),
                                                xnbufs=[(xn2, B_xn2), (xn2b, B_xn2b)])))
            for fp in range(2):
                for f2l in range(11):
                    f2 = fp * 11 + f2l

                    def cons_g(cc, n, pi):
                        P.op('act', lambda e, cc=cc, n=n, pi=pi: e.activation(out=sg[cc * 2 + n], in_=psb[pi], func=AF.Silu),
                             (PS[pi],), (B_sg[cc * 2 + n],))
                    fm_block_task(w_gate, 256 * f2, cons_g, src_hT=h2T, B_src=B_h2)

                    def cons_up(cc, n, pi, f2l=f2l):
                        P.op('dve', lambda e, cc=cc, n=n, pi=pi, f2l=f2l: e.tensor_tensor(
                            out=aT[:, 2 * f2l + cc, n * 512:(n + 1) * 512], in0=psb[pi], in1=sg[cc * 2 + n], op=ALU.mult),
                            (PS[pi], B_sg[cc * 2 + n]), (B_aT,))
                    fm_block_task(w_up, 256 * f2, cons_up, src_hT=h2T, B_src=B_h2)
                for dblk in range(8):
                    for rbk in range(2):
                        f0 = 16 * rbk
                        nf = 16 if rbk == 0 else 6

                        def ld(dblk=dblk, fp=fp, f0=f0, nf=nf):
                            sp = []
                            r0 = (fp * 22 + f0) * 128
                            r = 0
                            while r < nf * 128:
                                nr = min(1024, nf * 128 - r)
                                sp.append((w_down[r0 + r:r0 + r + nr, 256 * dblk:256 * dblk + 256], r // 128, 0))
                                r += nr
                            return sp

                        def comp(slot, b, dblk=dblk, fp=fp, f0=f0, nf=nf, rbk=rbk):
                            for tt in range(8):
                                def dmm(e, tt=tt):
                                    ins = None
                                    for c in range(nf):
                                        f = f0 + c
                                        ins = e.matmul(psb[tt][:, 0:256], aT[:, f, tt * 128:(tt + 1) * 128], slot[:, c, 0:256],
                                                       start=(f == 0), stop=(f == 21))
                                    return ins
                                pe_op(dmm, (B_aT,), (PS[tt],), (b,))
                                if rbk == 1:
                                    o = x1[:, tt, 256 * dblk:256 * dblk + 256]
                                    P.op('dve', lambda e, o=o, tt=tt: e.tensor_tensor(out=o, in0=o, in1=psb[tt][:, 0:256], op=ALU.add),
                                         (PS[tt], B_x1[tt]), (B_x1[tt],))
                                    if fp == 1 and dblk == 7:
                                        dma_sp(y_own[tt * 128:(tt + 1) * 128, :], x1[:, tt, :], (B_x1[tt],), ())
                        tasks.append((ld, comp))
            run_tasks()
        elif not dbg:
            raise RuntimeError("no output stage")
        P.barrier()

        with nc.Block() as block:
            def run(name, e):
                for waits, fn, inc in P.q[name]:
                    wait_all(e, waits)
                    if fn is None:
                        continue
                    ins = fn(e)
                    ins.then_inc(sems[inc[0]], inc[1])

            @block.tensor
            def _(e):
                run('pe', e)

            @block.scalar
            def _(e):
                run('act', e)

            @block.vector
            def _(e):
                run('dve', e)

            @block.gpsimd
            def _(e):
                run('pool', e)

            @block.sync
            def _(e):
                run('sp', e)
    return nc


def _bucket(d):
    d = np.asarray(d)
    df = np.maximum(d.astype(np.float32), np.float32(1.0))
    large = 16 + (np.log(df / np.float32(16)) / np.float32(math.log(8.0)) * np.float32(16)).astype(np.int32)
    large = np.minimum(large, 31)
    return np.where(d < 16, d, large)


def host_consts(r, inputs):
    c = {}
    ident = np.eye(128, dtype=np.float32)
    ones = np.ones((128, 128), np.float32)
    bd = np.zeros((128, 128), np.float32)
    bd[:64, :64] = 1
    bd[64:, 64:] = 1
    c["c_bf"] = np.concatenate([ident, ones, bd], axis=1)
    c["c_g1b"] = np.ascontiguousarray(np.broadcast_to(inputs["norm1_g"][0][None, :], (128, 2048)))
    c["c_g2b"] = np.ascontiguousarray(np.broadcast_to(inputs["norm2_g"][0][None, :], (128, 2048)))
    sm = np.zeros((128, 16), np.float32)
    sm[:, 0] = inputs["q_norm_g"][0]
    sm[:, 1] = inputs["k_norm_g"][0]
    sm[:, 2] = np.concatenate([inputs["idx_k_norm_g"][0]] * 2)
    sm[:, 4:12] = inputs["rel_bias"][31][None, :]
    c["c_sm"] = sm
    s_ = np.arange(128)
    c["c_tril"] = (s_[:, None] <= s_[None, :]).astype(np.float32)
    c["c_wspT"] = np.ascontiguousarray(inputs["w_spatial"][0].transpose(2, 0, 1))
    c["c_bsp"] = np.ascontiguousarray(np.broadcast_to(inputs["b_spatial"][0].reshape(1, 1024), (128, 1024)))
    c["c_sgug"] = np.ascontiguousarray(np.broadcast_to(inputs["sgu_norm_g"][0][None, :], (128, 1024)))
    c["c_rb"] = np.ascontiguousarray(inputs["rel_bias"])
    c["c_rb31"] = np.ascontiguousarray(np.broadcast_to(inputs["rel_bias"][31][None, :], (32, 8)))
    oh = np.zeros((32, 2, 512), np.float32)
    cneg = np.zeros((128, 2, 256), np.float32)
    tq = np.arange(128)
    for q in range(2):
        p = (q + r) % 2
        u = np.arange(511)
        d = u - 127 + 128 * (p - 1)
        ok = d >= 0
        b = _bucket(np.maximum(d, 0))
        oh[b[ok], q, u[ok]] = 1.0
        mine = 2 * 0 + p
        for jb in range(2):
            if jb < p:
                m = np.zeros((128, 128), bool)
            elif jb == p:
                m = tq[None, :] > tq[:, None]
            else:
                m = np.ones((128, 128), bool)
            cneg[:, q, jb * 128:(jb + 1) * 128] = np.where(m, np.float32(-1e30), np.float32(0))
    c["c_oh"] = oh
    c["c_ones32"] = np.ones((32, 128), np.float32)
    c["c_cneg"] = cneg
    return c


OWN = {0: [0, 3, 4, 7, 8, 11, 12, 15], 1: [1, 2, 5, 6, 9, 10, 13, 14]}
_cache = {}


def make_in_maps(inputs, cores):
    maps = []
    x = np.asarray(inputs["x"], np.float32)
    shared = {
        "w_in": np.ascontiguousarray(inputs["w_in"][0]), "w_pa": np.ascontiguousarray(inputs["w_proj_a"][0]),
        "w_pb": np.ascontiguousarray(inputs["w_proj_b"][0]), "w_out": np.ascontiguousarray(inputs["w_out"][0]),
        "w_gate": np.ascontiguousarray(inputs["w_ffn_gate"][0]), "w_up": np.ascontiguousarray(inputs["w_ffn_up"][0]),
        "w_down": np.ascontiguousarray(inputs["w_ffn_down"][0]),
    }
    for cidx in cores:
        b, r = cidx // 2, cidx % 2
        m = dict(shared)
        m["x_all"] = np.ascontiguousarray(x[b])
        m["x_own"] = np.ascontiguousarray(x[b].reshape(16, 128, 2048)[OWN[r]].reshape(1024, 2048))
        m.update(host_consts(r, inputs))
        maps.append(m)
    return maps


def kernel(**inputs):
    inputs = {k: np.asarray(v) for k, v in inputs.items()}
    if "nc" not in _cache:
        _cache["nc"] = build()
    nc = _cache["nc"]
    cores = list(range(8))
    maps = make_in_maps(inputs, cores)
    res = run_bass_kernel_spmd(nc, maps, core_ids=cores)
    out = np.zeros((4, 16, 128, 2048), np.float32)
    for cidx in cores:
        b, r = cidx // 2, cidx % 2
        y = np.asarray(res.results[cidx]["y_own"]).reshape(8, 128, 2048)
        out[b, OWN[r]] = y
    return out.reshape(4, 2048, 2048)
```
